# Optimizing a Trainium2 kernel written in Bass

```python
import math
import jax, jax.numpy as jnp
from jax import lax
import numpy as np

D_MODEL = 1024
BATCH = 8
SEQ = 4096
DEPTH = 1

SWA_HEAD_DIM = 64
SWA_HEADS = (D_MODEL // 2) // SWA_HEAD_DIM
SWA_WIDTH = SWA_HEADS * SWA_HEAD_DIM
SWA_PATTERNS = ((128, 1), (512, 4), (2048, 16))
SWA_ROT_DIM = SWA_HEAD_DIM // 4
MLA_NOPE_DIM = 128
MLA_ROPE_DIM = 64
MLA_V_DIM = 128
MLA_QK_DIM = MLA_NOPE_DIM + MLA_ROPE_DIM
MLA_HEADS = (D_MODEL // 2) // MLA_V_DIM
MLA_WIDTH = MLA_HEADS * MLA_V_DIM
MLA_Q_RANK = D_MODEL // 4
MLA_KV_RANK = D_MODEL // 8
Q_BLOCK = 128
MIX_WIDTH = SWA_WIDTH + MLA_WIDTH
IN_SPLITS = (SWA_WIDTH, 2 * SWA_WIDTH, 3 * SWA_WIDTH,
             3 * SWA_WIDTH + MLA_Q_RANK,
             3 * SWA_WIDTH + MLA_Q_RANK + MLA_KV_RANK)
IN_WIDTH = 3 * SWA_WIDTH + MLA_Q_RANK + MLA_KV_RANK + MLA_ROPE_DIM
D_FF = 2816
CONV_WIDTH = 3
ROPE_THETA = 500000.0
LN_EPS = 1e-5
RMS_EPS = 1e-6
NEG_INF = -1e30
DEEPNORM_ALPHA = (2.0 * DEPTH) ** 0.25
DEEPNORM_BETA = (8.0 * DEPTH) ** -0.25

kernel_name = 'hybrid_dilated_swa_mla_convffn_deepnorm'


def layer_norm(x, g, b):
    xf = x.astype(jnp.float32)
    mu = jnp.mean(xf, axis=-1, keepdims=True)
    xc = xf - mu
    var = jnp.mean(xc * xc, axis=-1, keepdims=True)
    y = xc * lax.rsqrt(var + LN_EPS) * g.astype(jnp.float32) + b.astype(jnp.float32)
    return y.astype(x.dtype)


def rms_norm(x, g, out_dtype):
    xf = x.astype(jnp.float32)
    y = xf * lax.rsqrt(jnp.mean(xf * xf, axis=-1, keepdims=True) + RMS_EPS) * g.astype(jnp.float32)
    return y.astype(out_dtype)


def rope(x, positions, rot_dim):
    half = rot_dim // 2
    inv_freq = jnp.power(jnp.float32(ROPE_THETA),
                         -jnp.arange(half, dtype=jnp.float32) * (2.0 / rot_dim))
    ang = positions.astype(jnp.float32)[:, :, None] * inv_freq
    cos = jnp.cos(ang)[:, :, None, :]
    sin = jnp.sin(ang)[:, :, None, :]
    xr = x[..., :rot_dim].astype(jnp.float32)
    x1, x2 = xr[..., :half], xr[..., half:]
    rot = jnp.concatenate([x1 * cos - x2 * sin, x2 * cos + x1 * sin], axis=-1).astype(x.dtype)
    return jnp.concatenate([rot, x[..., rot_dim:]], axis=-1)


def dilated_band_attention(q, k, v, dilation, n_side):
    B, S, H, Dh = q.shape
    L = S // dilation
    blk = math.gcd(L, n_side)
    nblk = L // blk
    span = blk + 2 * n_side

    def to_residue(t):
        return t.reshape(B, L, dilation, H, Dh).transpose(0, 2, 1, 3, 4).reshape(B * dilation, L, H, Dh)

    def from_residue(t):
        tail = t.shape[3:]
        t = t.reshape((B, dilation, L) + tail)
        t = t.transpose((0, 2, 1) + tuple(range(3, t.ndim)))
        return t.reshape((B, S) + tail)

    qr = to_residue(q).reshape(B * dilation, nblk, blk, H, Dh).astype(jnp.float32)
    pad = ((0, 0), (n_side, n_side), (0, 0), (0, 0))
    kp = jnp.pad(to_residue(k), pad)
    vp = jnp.pad(to_residue(v), pad)
    key_idx = jnp.arange(nblk)[:, None] * blk + jnp.arange(span)[None, :]
    kb = kp[:, key_idx].astype(jnp.float32)
    vb = vp[:, key_idx].astype(jnp.float32)
    q_pos = jnp.arange(nblk)[:, None] * blk + jnp.arange(blk)[None, :]
    k_pos = key_idx - n_side
    valid = ((jnp.abs(q_pos[:, :, None] - k_pos[:, None, :]) <= n_side)
             & (k_pos[:, None, :] >= 0) & (k_pos[:, None, :] < L))

    s = jnp.einsum('znqhd,znkhd->znhqk', qr, kb) * (Dh ** -0.5)
    s = jnp.where(valid[None, :, None, :, :], s, NEG_INF)
    m = jnp.max(s, axis=-1, keepdims=True)
    p = jnp.exp(s - m)
    den = jnp.sum(p, axis=-1)
    o = jnp.einsum('znhqk,znkhd->znqhd', p, vb)
    o = o / den.transpose(0, 1, 3, 2)[..., None]
    lse = (m[..., 0] + jnp.log(den)).transpose(0, 1, 3, 2)
    return from_residue(o), from_residue(lse)


def dilated_mixture_attention(q, k, v):
    outs, lses = [], []
    for window, dilation in SWA_PATTERNS:
        o, lse = dilated_band_attention(q, k, v, dilation, window // (2 * dilation))
        outs.append(o)
        lses.append(lse)
    w = jax.nn.softmax(jnp.stack(lses, axis=0), axis=0)
    return jnp.einsum('pbsh,pbshd->bshd', w, jnp.stack(outs, axis=0))


def mla_attention(q, k, v):
    B, S, H, Dq = q.shape
    nb = S // Q_BLOCK
    qb = q.reshape(B, nb, Q_BLOCK, H, Dq).transpose(1, 0, 2, 3, 4)
    kf = k.astype(jnp.float32)
    vf = v.astype(jnp.float32)
    scale = Dq ** -0.5

    def block(qi):
        s = jnp.einsum('bqhd,bkhd->bhqk', qi.astype(jnp.float32), kf) * scale
        p = jax.nn.softmax(s, axis=-1)
        return jnp.einsum('bhqk,bkhd->bqhd', p, vf)

    o = lax.map(block, qb)
    return o.transpose(1, 0, 2, 3, 4).reshape(B, S, H, v.shape[-1])


def hybrid_layer(x, positions, w_in, q_norm_g, w_uq, kv_norm_g, w_ukv, out_norm_g, w_o,
                 ln1_g, ln1_b, w_up, conv_w, conv_b, w_down, ln2_g, ln2_b):
    B, S, _ = x.shape
    dt = x.dtype
    h = x @ w_in
    q_a, k_a, v_a, c_q, c_kv, k_rope = jnp.split(h, list(IN_SPLITS), axis=-1)

    q_a = rope(q_a.reshape(B, S, SWA_HEADS, SWA_HEAD_DIM), positions, SWA_ROT_DIM)
    k_a = rope(k_a.reshape(B, S, SWA_HEADS, SWA_HEAD_DIM), positions, SWA_ROT_DIM)
    v_a = v_a.reshape(B, S, SWA_HEADS, SWA_HEAD_DIM)
    o_a = dilated_mixture_attention(q_a, k_a, v_a).reshape(B, S, SWA_WIDTH)

    q_b = (rms_norm(c_q, q_norm_g, dt) @ w_uq).reshape(B, S, MLA_HEADS, MLA_QK_DIM)
    q_nope, q_pe = q_b[..., :MLA_NOPE_DIM], q_b[..., MLA_NOPE_DIM:]
    q_pe = rope(q_pe, positions, MLA_ROPE_DIM)
    kv = (rms_norm(c_kv, kv_norm_g, dt) @ w_ukv).reshape(B, S, MLA_HEADS, MLA_NOPE_DIM + MLA_V_DIM)
    k_nope, v_b = kv[..., :MLA_NOPE_DIM], kv[..., MLA_NOPE_DIM:]
    k_pe = rope(k_rope[:, :, None, :], positions, MLA_ROPE_DIM)
    k_pe = jnp.broadcast_to(k_pe, (B, S, MLA_HEADS, MLA_ROPE_DIM))
    q_full = jnp.concatenate([q_nope, q_pe], axis=-1)
    k_full = jnp.concatenate([k_nope, k_pe], axis=-1)
    o_b = mla_attention(q_full, k_full, v_b).reshape(B, S, MLA_WIDTH)

    o = jnp.concatenate([rms_norm(o_a, out_norm_g[:SWA_WIDTH], dt),
                         rms_norm(o_b, out_norm_g[SWA_WIDTH:], dt)], axis=-1)
    x = layer_norm(DEEPNORM_ALPHA * x + o @ w_o, ln1_g, ln1_b)

    u = x @ w_up
    half = CONV_WIDTH // 2
    up = jnp.pad(u, ((0, 0), (half, half), (0, 0)))
    uc = conv_b
    for t in range(CONV_WIDTH):
        uc = uc + up[:, t:t + S] * conv_w[t]
    gate, val = uc[..., :D_FF], uc[..., D_FF:]
    y = (jax.nn.silu(gate) * val) @ w_down
    x = layer_norm(DEEPNORM_ALPHA * x + y, ln2_g, ln2_b)
    return x


def setup_inputs(seed: int = 0) -> dict:
    key = jax.random.key(seed)
    ks = jax.random.split(key, 24)
    f32 = jnp.float32
    beta = DEEPNORM_BETA

    def nrm(k, shape, scale):
        return jax.random.normal(k, shape, f32) * scale

    def gain(k, n):
        return 1.0 + 0.02 * jax.random.normal(k, (DEPTH, n), f32)

    x = jax.random.normal(ks[0], (BATCH, SEQ, D_MODEL), f32)
    offsets = jax.random.randint(ks[1], (BATCH, 1), 0, 1024, dtype=jnp.int32)
    positions = (jnp.arange(SEQ, dtype=jnp.int32)[None, :] + offsets).astype(jnp.int32)

    ln_emb_g = 1.0 + 0.02 * jax.random.normal(ks[2], (D_MODEL,), f32)
    ln_emb_b = 0.02 * jax.random.normal(ks[3], (D_MODEL,), f32)

    in_col_scale = jnp.concatenate([
        jnp.ones((2 * SWA_WIDTH,), f32), jnp.full((SWA_WIDTH,), beta, f32),
        jnp.ones((MLA_Q_RANK + MLA_KV_RANK + MLA_ROPE_DIM,), f32)])
    w_in = nrm(ks[4], (DEPTH, D_MODEL, IN_WIDTH), D_MODEL ** -0.5) * in_col_scale
    q_norm_g = gain(ks[5], MLA_Q_RANK)
    w_uq = nrm(ks[6], (DEPTH, MLA_Q_RANK, MLA_HEADS * MLA_QK_DIM), MLA_Q_RANK ** -0.5)
    kv_norm_g = gain(ks[7], MLA_KV_RANK)
    ukv_scale = jnp.tile(jnp.concatenate([jnp.ones((MLA_NOPE_DIM,), f32),
                                          jnp.full((MLA_V_DIM,), beta, f32)]), MLA_HEADS)
    w_ukv = nrm(ks[8], (DEPTH, MLA_KV_RANK, MLA_HEADS * (MLA_NOPE_DIM + MLA_V_DIM)),
                MLA_KV_RANK ** -0.5) * ukv_scale
    out_norm_g = gain(ks[9], MIX_WIDTH)
    w_o = nrm(ks[10], (DEPTH, MIX_WIDTH, D_MODEL), beta * MIX_WIDTH ** -0.5)
    ln1_g = gain(ks[11], D_MODEL)
    ln1_b = 0.02 * jax.random.normal(ks[12], (DEPTH, D_MODEL), f32)
    w_up = nrm(ks[13], (DEPTH, D_MODEL, 2 * D_FF), beta * D_MODEL ** -0.5)
    conv_w = nrm(ks[14], (DEPTH, CONV_WIDTH, 2 * D_FF), CONV_WIDTH ** -0.5)
    conv_b = 0.01 * jax.random.normal(ks[15], (DEPTH, 2 * D_FF), f32)
    w_down = nrm(ks[16], (DEPTH, D_FF, D_MODEL), beta * D_FF ** -0.5)
    ln2_g = gain(ks[17], D_MODEL)
    ln2_b = 0.02 * jax.random.normal(ks[18], (DEPTH, D_MODEL), f32)
    return {'x': x, 'positions': positions, 'ln_emb_g': ln_emb_g, 'ln_emb_b': ln_emb_b,
            'w_in': w_in, 'q_norm_g': q_norm_g, 'w_uq': w_uq, 'kv_norm_g': kv_norm_g,
            'w_ukv': w_ukv, 'out_norm_g': out_norm_g, 'w_o': w_o, 'ln1_g': ln1_g, 'ln1_b': ln1_b,
            'w_up': w_up, 'conv_w': conv_w, 'conv_b': conv_b, 'w_down': w_down,
            'ln2_g': ln2_g, 'ln2_b': ln2_b}


def reference(x, positions, ln_emb_g, ln_emb_b, w_in, q_norm_g, w_uq, kv_norm_g, w_ukv,
              out_norm_g, w_o, ln1_g, ln1_b, w_up, conv_w, conv_b, w_down, ln2_g, ln2_b):
    x = layer_norm(x, ln_emb_g, ln_emb_b)
    for l in range(DEPTH):
        x = hybrid_layer(x, positions, w_in[l], q_norm_g[l], w_uq[l], kv_norm_g[l], w_ukv[l],
                         out_norm_g[l], w_o[l], ln1_g[l], ln1_b[l], w_up[l], conv_w[l],
                         conv_b[l], w_down[l], ln2_g[l], ln2_b[l])
    return x
```

```python
import math
import os
from contextlib import ExitStack

import numpy as np
import ml_dtypes
import concourse.bass as bass
import concourse.mybir as mybir
from concourse.bass_utils import run_bass_kernel_spmd

F32 = mybir.dt.float32
BF16 = mybir.dt.bfloat16
I32 = mybir.dt.int32
AF = mybir.ActivationFunctionType
ALU = mybir.AluOpType

S = 4096
D = 1024
NT = S // 128
DFF = 2816
NFF = DFF // 128
LN_EPS = 1e-5
RMS_EPS = 1e-6
ALPHA = 2.0 ** 0.25
THETA = 500000.0
PI = math.pi
MAGIC = 12582912.0
C1 = 6.28125
C2 = 2.0 * PI - C1
PATTERNS = (1, 4, 16)
KPAD = 1024


class Clock:
    def __init__(self, sem, name):
        self.sem = sem
        self.val = 0
        self.name = name
        self.wp = []
        self.rp = []


class Buf:
    __slots__ = ("name", "w", "r")

    def __init__(self, name):
        self.name = name
        self.w = None
        self.r = {}


class Eng:
    def __init__(self, name, eng, clock):
        self.name = name
        self.eng = eng
        self.clock = clock
        self.seen = {}


class Kb:
    def __init__(self, nc, es):
        self.nc = nc
        self.es = es
        self.E = {}
        for name, eng in (("pe", nc.tensor), ("act", nc.scalar), ("dve", nc.vector),
                          ("pool", nc.gpsimd), ("sp", nc.sync)):
            sem = es.enter_context(nc.semaphore("s_" + name))
            self.E[name] = Eng(name, eng, Clock(sem, name))
        self.clocks = [e.clock for e in self.E.values()]
        self.nbuf = 0

    def dclock(self, name):
        sem = self.es.enter_context(self.nc.semaphore("d_" + name))
        c = Clock(sem, name)
        self.clocks.append(c)
        return c

    def buf(self, name=None):
        self.nbuf += 1
        return Buf(name or "b%d" % self.nbuf)

    def bufs(self, n, name=None):
        return [self.buf(name) for _ in range(n)]

    def _wait(self, e, c, v):
        if e.seen.get(c, 0) >= v:
            return
        e.eng.wait_ge(c.sem, v)
        e.seen[c] = v

    def op(self, en, fn, R=(), W=(), clock=None):
        e = self.E[en]
        need = {}

        def add(c, v):
            if e.seen.get(c, 0) >= v:
                return
            if need.get(c, 0) < v:
                need[c] = v

        for b in R:
            if b.w is not None:
                c, v = b.w
                if not (c is e.clock and en == "pe"):
                    add(c, v)
        for b in W:
            if b.w is not None:
                c, v = b.w
                if not (c is e.clock and en == "pe"):
                    add(c, v)
            for c, v in b.r.items():
                if c is e.clock:
                    continue
                add(c, v)
        items = list(need.items())
        embed = None
        if items and en != "pe":
            embed = items.pop()
        for c, v in items:
            self._wait(e, c, v)
        ins = fn()
        if embed is not None:
            ins._wait_ge(embed[0].sem, embed[1])
            e.seen[embed[0]] = embed[1]
        if clock is None:
            ck = e.clock
            step = 1
        else:
            ck = clock
            step = 16
        ins.then_inc(ck.sem, step)
        ck.val += step
        for b in R:
            if b.r.get(ck, 0) < ck.val:
                b.r[ck] = ck.val
        for b in W:
            b.w = (ck, ck.val)
            b.r = {}
        if clock is not None:
            ck.wp = [b for b in ck.wp if b.w is not None and b.w[0] is ck]
            for b in ck.wp:
                b.w = (ck, ck.val)
            ck.rp = [b for b in ck.rp if ck in b.r]
            for b in ck.rp:
                b.r[ck] = ck.val
            for b in W:
                if b not in ck.wp:
                    ck.wp.append(b)
            for b in R:
                if b not in ck.rp:
                    ck.rp.append(b)
        return ins

    def barrier(self):
        for e in self.E.values():
            for c in self.clocks:
                if c is e.clock:
                    continue
                if c.val > 0:
                    self._wait(e, c, c.val)

    def mm(self, out, lhsT, rhs, start, stop, R, W, **kw):
        nc = self.nc
        return self.op("pe", lambda: nc.tensor.matmul(out, lhsT, rhs, start=start, stop=stop, **kw), R, W)

    def tr(self, out, in_, ident, R, W):
        nc = self.nc
        return self.op("pe", lambda: nc.tensor.transpose(out, in_, ident), R, W)

    def act(self, out, in_, func, R, W, **kw):
        nc = self.nc
        return self.op("act", lambda: nc.scalar.activation(out, in_, func, **kw), R, W)

    def dma(self, out, in_, R, W, clock, q="sp"):
        e = self.E[q]
        return self.op(q, lambda: e.eng.dma_start(out=out, in_=in_), R, W, clock=clock)

    def tt(self, en, out, a, b, op, R, W):
        e = self.E[en]
        return self.op(en, lambda: e.eng.tensor_tensor(out, a, b, op), R, W)

    def ts(self, en, out, a, s1, s2, op0, op1, R, W):
        e = self.E[en]
        if op1 is None:
            return self.op(en, lambda: e.eng.tensor_scalar(out, a, s1, None, op0), R, W)
        return self.op(en, lambda: e.eng.tensor_scalar(out, a, s1, s2, op0, op1), R, W)

    def stt(self, en, out, a, s, b, op0, op1, R, W):
        e = self.E[en]
        return self.op(en, lambda: e.eng.scalar_tensor_tensor(out, a, s, b, op0, op1), R, W)

    def cp(self, en, out, in_, R, W):
        e = self.E[en]
        if en == "act":
            return self.op(en, lambda: e.eng.copy(out, in_), R, W)
        return self.op(en, lambda: e.eng.tensor_copy(out, in_), R, W)

    def memset(self, en, ap, val, W):
        e = self.E[en]
        return self.op(en, lambda: e.eng.memset(ap, val), (), W)


class Rot:
    def __init__(self, items):
        self.items = items
        self.i = 0

    def next(self):
        it = self.items[self.i % len(self.items)]
        self.i += 1
        return it


def interleave(*lists):
    out = []
    n = max(len(l) for l in lists)
    for i in range(n):
        for l in lists:
            k0 = (i * len(l)) // n
            k1 = ((i + 1) * len(l)) // n
            out.extend(l[k0:k1])
    return out


def build_program(debug=None, phases=(1, 2, 3, 4, 5)):
    nc = bass.Bass("TRN2", target_bir_lowering=False)
    es = ExitStack()
    dbg_kind = "ExternalOutput" if debug else "Internal"

    def din(name, shape, dt):
        return nc.dram_tensor(name, list(shape), dt, kind="ExternalInput").ap()

    def dscr(name, shape, dt):
        return nc.dram_tensor(name, list(shape), dt, kind=dbg_kind).ap()

    x_d = din("x", [S, D], F32)
    pos_d = din("pos", [S], I32)
    lneg_d = din("ln_emb_g", [D], F32)
    lneb_d = din("ln_emb_b", [D], F32)
    win_d = din("w_in", [D, 1984], F32)
    qng_d = din("q_norm_g", [256], F32)
    wuq_d = din("w_uq", [256, 768], F32)
    kvng_d = din("kv_norm_g", [128], F32)
    wukv_d = din("w_ukv", [128, 1024], F32)
    ong_d = din("out_norm_g", [D], F32)
    wo_d = din("w_o", [D, D], F32)
    ln1g_d = din("ln1_g", [D], F32)
    ln1b_d = din("ln1_b", [D], F32)
    wup_d = din("w_up", [D, 2 * DFF], F32)
    cw_d = din("conv_w", [3, 2 * DFF], F32)
    cb_d = din("conv_b", [2 * DFF], F32)
    wdn_d = din("w_down", [DFF, D], F32)
    ln2g_d = din("ln2_g", [D], F32)
    ln2b_d = din("ln2_b", [D], F32)
    ident_d = din("c_ident", [128, 128], BF16)
    ones_d = din("c_ones", [128, 128], BF16)
    mask_d = din("c_mask", [128, 256], BF16)
    rc_d = din("c_rope", [128, 8], F32)
    out_d = nc.dram_tensor("out", [S, D], F32, kind="ExternalOutput").ap()

    qkT_d = dscr("s_qkT", [8, 128, S], BF16)
    va_d = dscr("s_va", [S, 8, 65], BF16)
    qnT_d = dscr("s_qnT", [4, 128, S], BF16)
    qpT_d = dscr("s_qpT", [2, 128, S], BF16)
    knT_d = dscr("s_knT", [4, 128, S], BF16)
    kpT_d = dscr("s_kpT", [64, S], BF16)
    vb_d = dscr("s_vb", [S, 512], BF16)
    x1_d = dscr("s_x1", [S, D], F32)
    oT_d = dscr("s_oT", [8, 128, S], BF16)
    dbg_xn = dscr("dbg_xn", [S, D], BF16) if debug else None
    xn_d = dscr("s_xn", [S, D], F32)
    wupb_d = dscr("s_wupb", [2 * NFF, 128, 8, 128], BF16)

    k = Kb(nc, es)
    Bwupb = k.buf("wupb")
    ckwup = k.dclock("wup")

    def sb(stack, name, shape, dt):
        return stack.enter_context(nc.sbuf_tensor(name, list(shape), dt))

    def ps(stack, name, shape, dt):
        return stack.enter_context(nc.psum_tensor(name, list(shape), dt))

    ident = sb(es, "ident", [128, 128], BF16)
    ones = sb(es, "ones", [128, 128], BF16)
    mask = sb(es, "mask", [128, 256], BF16)
    rc = sb(es, "rc", [128, 8], F32)
    Bc = k.buf("consts")
    ck0 = k.dclock("c0")
    k.dma(ident[:], ident_d, (), [Bc], ck0)
    k.dma(ones[:], ones_d, (), [Bc], ck0)
    k.dma(mask[:], mask_d, (), [Bc], ck0)
    k.dma(rc[:], rc_d, (), [Bc], ck0)
    epsln = sb(es, "epsln", [128, 1], F32)
    epsrms = sb(es, "epsrms", [128, 1], F32)
    halfpi = sb(es, "halfpi", [128, 1], F32)
    k.memset("dve", epsln[:], LN_EPS, [Bc])
    k.memset("dve", epsrms[:], RMS_EPS, [Bc])
    k.memset("dve", halfpi[:], PI / 2, [Bc])
    k.memset("dve", halfpi[:], PI / 2, [Bc])

    ldck = [k.dclock("ld%d" % i) for i in range(8)]
    stck = [k.dclock("st%d" % i) for i in range(8)]

    def layer_norm_g(stack_tiles, src, dst, g_t, b_t, Bsrc, Bdst, out_bf=None, Bbf=None):
        st, mv, rs, nmr = stack_tiles
        Bs = stack_tiles_buf[id(st)]
        for hh in range(2):
            k.op("dve", lambda hh=hh: nc.vector.bn_stats(st[:, hh, :], src[:, hh * 512:(hh + 1) * 512]),
                 [Bsrc], [Bs])
        yield
        k.op("dve", lambda: nc.vector.bn_aggr(mv[:], st[:].rearrange("p a b -> p (a b)")), [Bs], [Bs])
        yield
        k.act(rs[:], mv[:, 1:2], AF.Ln, [Bs, Bc], [Bs], bias=epsln[:], scale=1.0)
        k.stt("dve", dst, src, mv[:, 0:1], g_t[:], ALU.subtract, ALU.mult, [Bsrc, Bs, Bc], [Bdst])
        yield
        k.act(rs[:], rs[:], AF.Exp, [Bs], [Bs], scale=-0.5)
        yield
        if out_bf is None:
            k.stt("dve", dst, dst, rs[:], b_t[:], ALU.mult, ALU.add, [Bdst, Bs, Bc], [Bdst])
        else:
            k.stt("dve", out_bf, dst, rs[:], b_t[:], ALU.mult, ALU.add, [Bdst, Bs, Bc], [Bbf])
        yield

    def run_mix(main, sides, width):
        sides = list(sides)
        active = []
        while main is not None or active or sides:
            while len(active) < width and sides:
                active.append(sides.pop(0))
            if main is not None:
                try:
                    next(main)
                except StopIteration:
                    main = None
            for g in list(active):
                try:
                    next(g)
                except StopIteration:
                    active.remove(g)

    stack_tiles_buf = {}

    def big_rsqrt(out, in_, R, W, scale):
        k.act(out, in_, AF.Ln, list(R) + [Bc], W, bias=epsrms[:], scale=scale)
        k.act(out, out, AF.Exp, W, W, scale=-0.5)

    def big_recip(out, in_, R, W):
        k.act(out, in_, AF.Ln, R, W)
        k.act(out, out, AF.Exp, W, W, scale=-1.0)

    def ln_scratch(stack, name):
        st = sb(stack, name + "_st", [128, 2, 6], F32)
        mv = sb(stack, name + "_mv", [128, 2], F32)
        rs = sb(stack, name + "_rs", [128, 1], F32)
        nmr = sb(stack, name + "_nm", [128, 1], F32)
        stack_tiles_buf[id(st)] = k.buf(name)
        return (st, mv, rs, nmr)

    def bcast_vec(stack, name, src_d, n):
        t = sb(stack, name, [128, n], F32)
        k.dma(t[:], src_d.partition_broadcast(128), (), [Bc], ck0)
        return t


    if 1 in phases:
        with ExitStack() as p1:
            lneg = bcast_vec(p1, "lneg", lneg_d, D)
            lneb = bcast_vec(p1, "lneb", lneb_d, D)
            win = sb(p1, "win", [128, 8, 1984], BF16)
            winp = sb(p1, "winp", [128, 8, 1024], BF16)
            winpk = sb(p1, "winpk", [128, 8, 64], BF16)
            Bw = k.buf("w1")
            ck0p = k.dclock("c0p")
            for kc in range(8):
                k.dma(win[:, kc, :], win_d[kc * 128:(kc + 1) * 128, :], (), [Bw], ck0p, q="pool")
            k.memset("pool", winp[:], 0.0, [Bw])
            wv = win[:, :, 0:1024].rearrange("p k (h d) -> p k h d", d=64)
            wpv = winp[:].rearrange("p k (h d) -> p k h d", d=64)
            for kc in range(8):
                k.cp("pool", wpv[:, kc, :, 0:8], wv[:, kc, :, 8:16], [Bw], [Bw])
                k.cp("pool", wpv[:, kc, :, 8:16], wv[:, kc, :, 0:8], [Bw], [Bw])
            k.cp("pool", winpk[:, :, 0:32], win[:, :, 1952:1984], [Bw], [Bw])
            k.cp("pool", winpk[:, :, 32:64], win[:, :, 1920:1952], [Bw], [Bw])
            qng = sb(p1, "qng", [128, 2], F32)
            wuq = sb(p1, "wuq", [128, 2, 768], BF16)
            wqn = sb(p1, "wqn", [128, 2, 4, 128], BF16)
            wqp = sb(p1, "wqp", [128, 2, 4, 64], BF16)
            wqpp = sb(p1, "wqpp", [128, 2, 4, 64], BF16)
            kvng = sb(p1, "kvng", [128, 1], F32)
            wkn = sb(p1, "wkn", [128, 4, 128], BF16)
            wvb = sb(p1, "wvb", [128, 4, 128], BF16)
            CA = sb(p1, "CA", [128, S], BF16)
            SA = sb(p1, "SA", [128, S], BF16)
            CB = sb(p1, "CB", [128, S], BF16)
            SB_ = sb(p1, "SB", [128, S], BF16)
            Btab = k.buf("tab")
            pw = ExitStack()
            wuq_f = sb(pw, "wuq_f", [128, 2, 768], F32)
            wukv_f = sb(pw, "wukv_f", [128, 1024], F32)
            posi = sb(pw, "posi", [128, S], I32)
            posf = sb(pw, "posf", [128, S], F32)
            ang = sb(pw, "ang", [128, S], F32)
            kk = sb(pw, "kk", [128, S], F32)
            Bt = k.buf("tabtmp")
            k.dma(posi[:], pos_d.partition_broadcast(128), (), [Bt], ck0)
            k.dma(wuq_f[:], wuq_d.rearrange("(k p) n -> p k n", p=128), (), [Bw], ck0)
            for kc in range(2):
                k.dma(qng[:, kc:kc + 1], qng_d[kc * 128:(kc + 1) * 128].rearrange("(p o) -> p o", o=1), (), [Bw], ck0)
            k.dma(wukv_f[:], wukv_d, (), [Bw], ck0)
            k.dma(kvng[:], kvng_d.rearrange("(p o) -> p o", o=1), (), [Bw], ck0)
            k.cp("dve", posf[:], posi[:], [Bt], [Bt])
            for (fc, sc, Ct, St) in ((0, 1, CA, SA), (2, 3, CB, SB_)):
                k.ts("dve", ang[:], posf[:], rc[:, fc:fc + 1], None, ALU.mult, None, [Bt, Bc], [Bt])
                k.ts("dve", kk[:], ang[:], 1.0 / (2 * PI), MAGIC, ALU.mult, ALU.add, [Bt], [Bt])
                k.ts("dve", kk[:], kk[:], MAGIC, None, ALU.subtract, None, [Bt], [Bt])
                k.stt("dve", ang[:], kk[:], -C1, ang[:], ALU.mult, ALU.add, [Bt], [Bt])
                k.stt("dve", ang[:], kk[:], -C2, ang[:], ALU.mult, ALU.add, [Bt], [Bt])
                k.ts("dve", ang[:], ang[:], PI, -PI, ALU.min, ALU.max, [Bt], [Bt])
                k.act(St[:], ang[:], AF.Sin, [Bt, Bc], [Btab], scale=rc[:, sc:sc + 1])
                k.act(kk[:], ang[:], AF.Abs, [Bt], [Bt])
                k.act(Ct[:], kk[:], AF.Sin, [Bt, Bc], [Btab], scale=-1.0, bias=halfpi[:])
            for kc in range(2):
                k.ts("dve", wuq[:, kc, :], wuq_f[:, kc, :], qng[:, kc:kc + 1], None, ALU.mult, None, [Bw], [Bw])
            wuq4 = wuq[:].rearrange("p k (h d) -> p k h d", d=192)
            for kc in range(2):
                k.cp("dve", wqn[:, kc, :, :], wuq4[:, kc, :, 0:128], [Bw], [Bw])
                k.cp("dve", wqp[:, kc, :, :], wuq4[:, kc, :, 128:192], [Bw], [Bw])
                k.cp("dve", wqpp[:, kc, :, 0:32], wuq4[:, kc, :, 160:192], [Bw], [Bw])
                k.cp("dve", wqpp[:, kc, :, 32:64], wuq4[:, kc, :, 128:160], [Bw], [Bw])
            wukv4 = wukv_f[:].rearrange("p (h d) -> p h d", d=256)
            k.ts("dve", wkn[:], wukv4[:, :, 0:128], kvng[:, 0:1], None, ALU.mult, None, [Bw], [Bw])
            k.ts("dve", wvb[:], wukv4[:, :, 128:256], kvng[:, 0:1], None, ALU.mult, None, [Bw], [Bw])
            k.barrier()
            pw.close()

            NXB = 2
            xin = [sb(p1, "xin%d" % i, [128, D], F32) for i in range(NXB)]
            Bxin = k.bufs(NXB, "xin")
            xnb = [sb(p1, "xnb%d" % i, [128, D], BF16) for i in range(2)]
            Bxnb = k.bufs(2, "xnb")
            lnsc = [ln_scratch(p1, "ln%d" % i) for i in range(2)]
            xnT = [sb(p1, "xnT%d" % i, [128, 8, 512], BF16) for i in range(2)]
            BxnT = [k.bufs(4, "xnT") for _ in range(2)]
            tp_ps = [ps(p1, "tp%d" % i, [128, 8, 128], BF16) for i in range(2)]
            Btp = k.bufs(2, "tp")
            NB = 6
            banks = Rot([(ps(p1, "bk%d" % i, [128, 512], F32), k.buf("bk%d" % i)) for i in range(NB)])
            qk_st = [sb(p1, "qk_st%d" % i, [128, 8, 512], BF16) for i in range(2)]
            qn_st = [sb(p1, "qn_st%d" % i, [128, 4, 512], BF16) for i in range(2)]
            qp_st = [sb(p1, "qp_st%d" % i, [128, 2, 512], BF16) for i in range(2)]
            kn_st = [sb(p1, "kn_st%d" % i, [128, 4, 512], BF16) for i in range(2)]
            kp_st = [sb(p1, "kp_st%d" % i, [64, 512], BF16) for i in range(2)]
            va_st = [sb(p1, "va_st%d" % i, [128, 4, 8, 65], BF16) for i in range(2)]
            vb_st = [sb(p1, "vb_st%d" % i, [128, 4, 512], BF16) for i in range(2)]
            Bqk = [k.bufs(8) for _ in range(2)]
            Bqn = [k.bufs(4) for _ in range(2)]
            Bqp = [k.bufs(2) for _ in range(2)]
            Bkn = [k.bufs(4) for _ in range(2)]
            Bkp = k.bufs(2)
            Bva = [k.bufs(4) for _ in range(2)]
            Bvb = [k.bufs(4) for _ in range(2)]
            for i in range(2):
                k.memset("pool", va_st[i][:, :, :, 64:65], 1.0, Bva[i])
            cq_bf = [sb(p1, "cq%d" % i, [128, 2, 512], BF16) for i in range(1)]
            ckv_bf = [sb(p1, "ckv%d" % i, [128, 512], BF16) for i in range(1)]
            sq_q = [sb(p1, "sqq%d" % i, [128, 2, 512], BF16) for i in range(1)]
            sq_kv = [sb(p1, "sqkv%d" % i, [128, 512], BF16) for i in range(1)]
            Bcq = [k.bufs(2) for _ in range(1)]
            Bckv = k.bufs(1)
            Bsqq = [k.bufs(2) for _ in range(1)]
            Bsqkv = k.bufs(1)
            rq = [sb(p1, "rq%d" % i, [128, 512], F32) for i in range(1)]
            rkv = [sb(p1, "rkv%d" % i, [128, 512], F32) for i in range(1)]
            Brq = k.bufs(1)
            Brkv = k.bufs(1)
            rkvc = [sb(p1, "rkvc%d" % i, [128, 4], F32) for i in range(1)]
            Brkvc = [k.bufs(4) for _ in range(1)]
            NTMP = 2
            tmpA = Rot([(sb(p1, "tmpA%d" % i, [128, 512], F32), k.buf()) for i in range(NTMP)])
            tmpB = Rot([(sb(p1, "tmpB%d" % i, [128, 512], F32), k.buf()) for i in range(NTMP)])

            def stageA(blk):
                gens = []
                sl = blk % 2
                for tt in range(4):
                    def g(tt=tt):
                        ti = blk * 4 + tt
                        xi = ti % NXB
                        li = ti % 2
                        k.dma(xin[xi][:], x_d[ti * 128:(ti + 1) * 128, :], (), [Bxin[xi]], ldck[xi])
                        yield
                        yield from layer_norm_g(lnsc[li], xin[xi][:], xin[xi][:], lneg, lneb, Bxin[xi], Bxin[xi])
                        k.dma(xn_d[ti * 128:(ti + 1) * 128, :], xin[xi][:], [Bxin[xi]], (), stck[2 + xi])
                        k.cp("act", xnb[li][:], xin[xi][:], [Bxin[xi]], [Bxnb[li]])
                        yield
                        if debug:
                            k.dma(dbg_xn[ti * 128:(ti + 1) * 128, :], xnb[li][:], [Bxnb[li]], (), stck[4 + li])
                        for kc in range(8):
                            k.tr(tp_ps[li][:, kc, :], xnb[li][:, kc * 128:(kc + 1) * 128], ident[:],
                                 [Bxnb[li], Bc], [Btp[li]])
                        yield
                        k.cp("act", xnT[sl][:, :, tt * 128:(tt + 1) * 128], tp_ps[li][:], [Btp[li]], [BxnT[sl][tt]])
                        yield
                    gens.append(g())
                return gens

            def rope_combine(hA, BA, hB, BB, Ct, St, t0, out, Bout, npart=128, mul=None, Bmul=None):
                tA, BtA = tmpA.next()
                tB, BtB = tmpB.next()
                k.tt("dve", tA[0:npart, :], hA, Ct[0:npart, t0:t0 + 512], ALU.mult, [BA, Btab], [BtA])
                k.tt("dve", tB[0:npart, :], hB, St[0:npart, t0:t0 + 512], ALU.mult, [BB, Btab], [BtB])
                if mul is None:
                    k.tt("pool", out, tA[0:npart, :], tB[0:npart, :], ALU.add, [BtA, BtB], [Bout])
                else:
                    k.tt("pool", tA[0:npart, :], tA[0:npart, :], tB[0:npart, :], ALU.add, [BtA, BtB], [BtA])
                    k.tt("pool", out, tA[0:npart, :], mul, ALU.mult, [BtA, Bmul], [Bout])

            def stageB(blk):
                items = []
                sl = blk % 2
                t0 = blk * 512
                RX = BxnT[sl]

                def proj(wt, c0, m, bank):
                    for kc in range(8):
                        k.mm(bank[0][0:m, :], wt[:, kc, c0:c0 + m], xnT[sl][:, kc, :], kc == 0, kc == 7,
                             RX + [Bw], [bank[1]])

                for c in range(8):
                    def f(c=c):
                        bA = banks.next()
                        bB = banks.next()
                        proj(win, c * 128, 128, bA)
                        proj(winp, c * 128, 128, bB)
                        rope_combine(bA[0][:], bA[1], bB[0][:], bB[1], CA, SA, t0, qk_st[sl][:, c, :], Bqk[sl][c])
                    items.append(f)
                for i in range(2):
                    def f(i=i):
                        b = banks.next()
                        proj(win, 1536 + i * 128, 128, b)
                        k.cp("act", cq_bf[0][:, i, :], b[0][:], [b[1]], [Bcq[0][i]])
                        k.act(sq_q[0][:, i, :], b[0][:], AF.Square, [b[1]], [Bsqq[0][i]])
                    items.append(f)

                def f():
                    b = banks.next()
                    proj(win, 1792, 128, b)
                    k.cp("act", ckv_bf[0][:], b[0][:], [b[1]], [Bckv[0]])
                    k.act(sq_kv[0][:], b[0][:], AF.Square, [b[1]], [Bsqkv[0]])
                items.append(f)

                def f():
                    bA = banks.next()
                    bB = banks.next()
                    proj(win, 1920, 64, bA)
                    for kc in range(8):
                        k.mm(bB[0][0:64, :], winpk[:, kc, :], xnT[sl][:, kc, :], kc == 0, kc == 7, RX + [Bw], [bB[1]])
                    rope_combine(bA[0][0:64, :], bA[1], bB[0][0:64, :], bB[1], CB, SB_, t0, kp_st[sl][:], Bkp[sl],
                                 npart=64)
                items.append(f)

                for tt in range(4):
                    def f(tt=tt):
                        b = banks.next()
                        for kc in range(8):
                            k.mm(b[0][:], xnT[sl][:, kc, tt * 128:(tt + 1) * 128], win[:, kc, 1024:1536],
                                 kc == 0, kc == 7, [RX[tt], Bw], [b[1]])
                        k.cp("act", va_st[sl][:, tt, :, 0:64], b[0][:].rearrange("p (h d) -> p h d", d=64),
                             [b[1]], [Bva[sl][tt]])
                    items.append(f)

                def f():
                    b = banks.next()
                    for i in range(2):
                        k.mm(b[0][:], ones[:], sq_q[0][:, i, :], i == 0, i == 1, [Bsqq[0][i], Bc], [b[1]])
                    big_rsqrt(rq[0][:], b[0][:], [b[1]], [Brq[0]], 1.0 / 256)
                    b2 = banks.next()
                    k.mm(b2[0][:], ones[:], sq_kv[0][:], True, True, [Bsqkv[0], Bc], [b2[1]])
                    big_rsqrt(rkv[0][:], b2[0][:], [b2[1]], [Brkv[0]], 1.0 / 128)
                items.append(f)

                for h in range(4):
                    def f(h=h):
                        b = banks.next()
                        for kc in range(2):
                            k.mm(b[0][:], wqn[:, kc, h, :], cq_bf[0][:, kc, :], kc == 0, kc == 1,
                                 [Bcq[0][kc], Bw], [b[1]])
                        k.tt("dve", qn_st[sl][:, h, :], b[0][:], rq[0][:], ALU.mult, [b[1], Brq[0]], [Bqn[sl][h]])
                    items.append(f)
                for i in range(2):
                    def f(i=i):
                        bA = banks.next()
                        bB = banks.next()
                        for kc in range(2):
                            k.mm(bA[0][:], wqp[:, kc, 2 * i:2 * i + 2, :], cq_bf[0][:, kc, :], kc == 0, kc == 1,
                                 [Bcq[0][kc], Bw], [bA[1]])
                        for kc in range(2):
                            k.mm(bB[0][:], wqpp[:, kc, 2 * i:2 * i + 2, :], cq_bf[0][:, kc, :], kc == 0, kc == 1,
                                 [Bcq[0][kc], Bw], [bB[1]])
                        rope_combine(bA[0][:], bA[1], bB[0][:], bB[1], CB, SB_, t0, qp_st[sl][:, i, :], Bqp[sl][i],
                                     mul=rq[0][:], Bmul=Brq[0])
                    items.append(f)
                for h in range(4):
                    def f(h=h):
                        b = banks.next()
                        k.mm(b[0][:], wkn[:, h, :], ckv_bf[0][:], True, True, [Bckv[0], Bw], [b[1]])
                        k.tt("dve", kn_st[sl][:, h, :], b[0][:], rkv[0][:], ALU.mult, [b[1], Brkv[0]], [Bkn[sl][h]])
                    items.append(f)
                for tt in range(4):
                    def f(tt=tt):
                        b2 = banks.next()
                        k.mm(b2[0][:, 0:2], sq_kv[0][:, tt * 128:(tt + 1) * 128], ones[:, 0:2], True, True,
                             [Bsqkv[0], Bc], [b2[1]])
                        col = rkvc[0][:, tt:tt + 1]
                        big_rsqrt(col, b2[0][:, 0:1], [b2[1]], [Brkvc[0][tt]], 1.0 / 128)
                        b = banks.next()
                        k.mm(b[0][:], ckv_bf[0][:, tt * 128:(tt + 1) * 128], wvb[:].rearrange("p h d -> p (h d)"),
                             True, True, [Bckv[0], Bw], [b[1]])
                        k.act(vb_st[sl][:, tt, :], b[0][:], AF.Copy, [b[1], Brkvc[0][tt]], [Bvb[sl][tt]], scale=col)
                    items.append(f)
                return items

            def stores(blk):
                sl = blk % 2
                t0 = blk * 512
                cs = stck[sl]
                k.dma(qkT_d[:, :, t0:t0 + 512].rearrange("c p t -> p c t"), qk_st[sl][:], Bqk[sl], (), cs)
                k.dma(qnT_d[:, :, t0:t0 + 512].rearrange("c p t -> p c t"), qn_st[sl][:], Bqn[sl], (), cs)
                k.dma(qpT_d[:, :, t0:t0 + 512].rearrange("c p t -> p c t"), qp_st[sl][:], Bqp[sl], (), cs)
                k.dma(knT_d[:, :, t0:t0 + 512].rearrange("c p t -> p c t"), kn_st[sl][:], Bkn[sl], (), cs)
                k.dma(kpT_d[:, t0:t0 + 512], kp_st[sl][:], [Bkp[sl]], (), cs)
                k.dma(va_d[t0:t0 + 512].rearrange("(a p) h d -> p a h d", p=128), va_st[sl][:], Bva[sl], (), cs)
                k.dma(vb_d[t0:t0 + 512].rearrange("(a p) n -> p a n", p=128), vb_st[sl][:], Bvb[sl], (), cs)

            def genB(blk):
                for f in stageB(blk):
                    f()
                    yield

            run_mix(None, stageA(0), 2)
            for blk in range(8):
                A = stageA(blk + 1) if blk < 7 else []
                run_mix(genB(blk), A, 2)
                stores(blk)
            k.barrier()

    if 2 in phases:
        with ExitStack() as p2:
            SC2 = 0.125
            mask2 = sb(p2, "mask2", [128, 2, 256], BF16)
            Bm2 = k.buf("mask2")
            for hp in range(2):
                k.cp("pool", mask2[:, hp, :], mask[:], [Bc], [Bm2])
            onesf = sb(p2, "onesf", [128, 64], F32)
            k.memset("pool", onesf[:], 1.0, [Bm2])
            qc = [sb(p2, "a_q%d" % i, [128, S], BF16) for i in range(2)]
            kc_ = [sb(p2, "a_k%d" % i, [128, S + 2 * KPAD], BF16) for i in range(2)]
            Bqc = k.bufs(2)
            Bkc = k.bufs(2)
            for i in range(2):
                k.memset("pool", kc_[i][:, 0:KPAD], 0.0, [Bkc[i]])
                k.memset("pool", kc_[i][:, KPAD + S:], 0.0, [Bkc[i]])
            NTMAX = 48
            Vt = [sb(p2, "a_v%d" % i, [128, NTMAX, 130], BF16) for i in range(3)]
            BVt = k.bufs(3)
            Oaccs = [sb(p2, "a_oacc%d" % i, [65, 2, S], F32) for i in range(2)]
            BOaccs = [[[k.buf() for _ in range(16)] for _ in range(NT)] for _ in range(2)]
            BOacc = BOaccs[0]

            def oacc_bufs(t0, d, nq, BOacc):
                blocks = range(t0 // 128, (t0 + (nq - 1) * d) // 128 + 1)
                res = sorted(set((t0 + b * d) % 16 for b in range(16)))
                return [BOacc[bl][rr_] for bl in blocks for rr_ in res]
            S2 = [(ps(p2, "a_S%d" % i, [128, 2, 512], F32), k.buf()) for i in range(2)]
            Obk = Rot([(ps(p2, "a_O%d" % i, [128, 2, 256], F32), k.buf()) for i in range(3)])
            finb = (ps(p2, "a_fin", [128, 512], F32), k.buf())
            PT2 = Rot([(sb(p2, "a_PT%d" % i, [128, 2, 256], BF16), k.buf()) for i in range(6)])
            rdn2 = Rot([(sb(p2, "a_rd%d" % i, [64, 512], F32), k.buf()) for i in range(2)])
            oa_st = Rot([(sb(p2, "a_ost%d" % i, [64, 512], BF16), k.buf()) for i in range(2)])
            cka = [k.dclock("a%d" % i) for i in range(5)]

            def load_chunk(c):
                sl = c % 2
                k.dma(qc[sl][:], qkT_d[c], (), [Bqc[sl]], cka[sl])
                k.dma(kc_[sl][:, KPAD:KPAD + S], qkT_d[4 + c], (), [Bkc[sl]], cka[sl])

            vcount = [0]
            regcnt = [0]

            def load_v(c, d):
                sl = vcount[0] % 3
                vcount[0] += 1
                L = S // d
                nt = L // 128 + 1
                V4 = Vt[sl][:, 0:d * nt, :].rearrange("p (r m) e -> p r m e", m=nt)
                src = va_d[:, 2 * c:2 * c + 2, :].rearrange("(m p r) h e -> p r m (h e)", p=128, r=d)
                k.memset("pool", V4[0:64, :, 0, :], 0.0, [BVt[sl]])
                k.memset("pool", V4[64:128, :, nt - 1, :], 0.0, [BVt[sl]])
                if d <= nt - 1:
                    for r in range(d):
                        k.dma(V4[64:128, r, 0:nt - 1, :], src[0:64, r, :, :], (), [BVt[sl]], cka[2 + sl], q="pool")
                        k.dma(V4[0:64, r, 1:nt, :], src[64:128, r, :, :], (), [BVt[sl]], cka[2 + sl], q="pool")
                else:
                    for m in range(nt - 1):
                        k.dma(V4[64:128, :, m, :], src[0:64, :, m, :], (), [BVt[sl]], cka[2 + sl], q="pool")
                        k.dma(V4[0:64, :, m + 1, :], src[64:128, :, m, :], (), [BVt[sl]], cka[2 + sl], q="pool")
                return sl

            segs = [(c_, pi_) for c_ in range(4) for pi_ in range(3)]
            vslots = {}
            for si_ in range(2):
                vslots[si_] = load_v(segs[si_][0], PATTERNS[segs[si_][1]])
            fin_jobs = []
            load_chunk(0)
            for si, (c, pi) in enumerate(segs):
                d = PATTERNS[pi]
                sl = c % 2
                Oacc = Oaccs[c % 2]
                BOacc = BOaccs[c % 2]
                if pi == 0:
                    if c + 1 < 4:
                        load_chunk(c + 1)
                    k.memset("pool", Oacc[:], 0.0, [b_ for row in BOacc for b_ in row])
                if si + 2 < len(segs):
                    vslots[si + 2] = load_v(segs[si + 2][0], PATTERNS[segs[si + 2][1]])
                vsl = vslots.pop(si)
                if True:
                    L = S // d
                    nt = L // 128 + 1
                    tiles = [(r, m) for r in range(d) for m in range(nt)]
                    pend = {}
                    regs = {}

                    def emit_qk(i):
                        r, m = tiles[i]
                        b0 = 128 if m == 0 else 0
                        b1 = 128 if m == nt - 1 else 256
                        kstart = KPAD + (128 * m - 64) * d + r
                        qstart = (128 * (m - 1) + b0) * d + r
                        nq = b1 - b0
                        Sb2 = S2[i % 2]
                        for hp in range(2):
                            k.mm(Sb2[0][:, hp, b0:b1],
                                 kc_[sl][hp * 64:(hp + 1) * 64, kstart:kstart + 127 * d + 1:d],
                                 qc[sl][hp * 64:(hp + 1) * 64, qstart:qstart + (nq - 1) * d + 1:d],
                                 True, True, [Bkc[sl], Bqc[sl]], [Sb2[1]])
                        pt = PT2.next()
                        k.act(pt[0][:, :, b0:b1], Sb2[0][:, :, b0:b1], AF.Exp, [Sb2[1]], [pt[1]], scale=SC2)
                        k.tt("dve", pt[0][:, :, b0:b1], pt[0][:, :, b0:b1], mask2[:, :, b0:b1], ALU.mult,
                             [pt[1], Bm2], [pt[1]])
                        pend[i] = (pt, b0, b1)

                    def emit_pv(i):
                        r, m = tiles[i]
                        pt, b0, b1 = pend.pop(i)
                        j = r * nt + m
                        ob = Obk.next()
                        nq = b1 - b0
                        for hp in range(2):
                            k.mm(ob[0][0:65, hp, b0:b1], Vt[vsl][:, j, hp * 65:(hp + 1) * 65], pt[0][:, hp, b0:b1],
                                 True, True, [BVt[vsl], pt[1]], [ob[1]])
                        t0 = (128 * (m - 1) + b0) * d + r
                        dst = Oacc[:, :, t0:t0 + (nq - 1) * d + 1:d]
                        bufs_ = oacc_bufs(t0, d, nq, BOacc)
                        k.tt("dve", dst, ob[0][0:65, :, b0:b1], dst, ALU.add, [ob[1]] + bufs_, bufs_)

                    n = len(tiles)
                    SK = 3
                    for i in range(n + SK):
                        if i < n:
                            emit_qk(i)
                        if i >= SK:
                            emit_pv(i - SK)
                        if fin_jobs and i % 2 == 1:
                            fin_jobs.pop(0)()
                if pi == 2:
                    for hp in range(2):
                        for blk in range(8):
                            def fin(hp=hp, blk=blk, c=c, Oacc=Oacc, BOacc=BOacc):
                                t0 = blk * 512
                                fb = finb
                                fbufs = [BOacc[bl][rr_] for bl in range(blk * 4, blk * 4 + 4) for rr_ in range(16)]
                                k.mm(fb[0][0:64, :], onesf[64:65, 0:64], Oacc[64:65, hp, t0:t0 + 512], True, True,
                                     fbufs + [Bm2], [fb[1]])
                                rd = rdn2.next()
                                big_recip(rd[0][:], fb[0][0:64, :], [fb[1]], [rd[1]])
                                ost = oa_st.next()
                                k.tt("dve", ost[0][:], Oacc[0:64, hp, t0:t0 + 512], rd[0][:], ALU.mult, fbufs + [rd[1]], [ost[1]])
                                k.dma(oT_d[c][hp * 64:(hp + 1) * 64, t0:t0 + 512], ost[0][:], [ost[1]], (), stck[4 + (blk % 2)])
                            fin_jobs.append(fin)
            while fin_jobs:
                fin_jobs.pop(0)()
            k.barrier()

    if 3 in phases:
        with ExitStack() as p3:
            SC3 = 1.0 / math.sqrt(192.0)
            for ch in range(2 * NFF):
                k.dma(wupb_d[ch], wup_d[:, ch * 128:(ch + 1) * 128].rearrange("(k p) n -> p k n", p=128), (),
                      [Bwupb], ckwup, q="pool")
            qp3 = [sb(p3, "m_qp%d" % i, [128, S], BF16) for i in range(2)]
            kp2 = sb(p3, "m_kp2", [128, S], BF16)
            Bqp3 = k.bufs(2)
            Bkp3 = k.buf()
            ckm = k.dclock("m0")
            for i in range(2):
                k.dma(qp3[i][:], qpT_d[i], (), [Bqp3[i]], ckm)
            k.dma(kp2[0:64, :], kpT_d, (), [Bkp3], ckm)
            k.dma(kp2[64:128, :], kpT_d, (), [Bkp3], ckm)
            qn3 = [sb(p3, "m_qn%d" % i, [128, S], BF16) for i in range(2)]
            kn3 = [sb(p3, "m_kn%d" % i, [128, S], BF16) for i in range(2)]
            vb3 = [sb(p3, "m_vb%d" % i, [128, NT, 128], BF16) for i in range(2)]
            Bhd = [k.bufs(3) for _ in range(2)]
            ckh = [k.dclock("mh%d" % i) for i in range(2)]

            def load_head(h):
                sl = h % 2
                k.dma(qn3[sl][:], qnT_d[h], (), [Bhd[sl][0]], ckh[sl])
                k.dma(kn3[sl][:], knT_d[h], (), [Bhd[sl][1]], ckh[sl])
                k.dma(vb3[sl][:], vb_d[:, h * 128:(h + 1) * 128].rearrange("(t p) n -> p t n", p=128), (),
                      [Bhd[sl][2]], ckh[sl])

            Sb3 = Rot([(ps(p3, "m_S%d" % i, [128, 512], F32), k.buf()) for i in range(4)])
            Ob3 = Rot([(ps(p3, "m_O%d" % i, [128, 512], F32), k.buf()) for i in range(2)])
            Db3 = Rot([(ps(p3, "m_D%d" % i, [128, 512], F32), k.buf()) for i in range(2)])
            PT3 = Rot([(sb(p3, "m_PT%d" % i, [128, 512], BF16), k.buf()) for i in range(5)])
            rdn3 = Rot([(sb(p3, "m_rd%d" % i, [128, 512], F32), k.buf()) for i in range(2)])
            ob3 = [(sb(p3, "m_ob%d" % i, [128, 512], BF16), k.buf()) for i in range(2)]
            dacc = [[(sb(p3, "m_dacc%d_%d" % (i, a), [128, 512], F32), k.buf()) for a in range(4)] for i in range(2)]
            dsum = [(sb(p3, "m_dsum%d" % i, [128, 512], BF16), k.buf()) for i in range(2)]
            steps = [(h, qb, kt) for h in range(4) for qb in range(8) for kt in range(NT)]
            pend3 = {}
            cur = {}
            load_head(0)

            def emit_s(i):
                h, qb, kt = steps[i]
                sl = h % 2
                hp = h % 2
                Sbk = Sb3.next()
                pt = PT3.next()
                k.mm(Sbk[0][:], kn3[sl][:, kt * 128:(kt + 1) * 128], qn3[sl][:, qb * 512:(qb + 1) * 512],
                     True, False, [Bhd[sl][1], Bhd[sl][0]], [Sbk[1]])
                k.mm(Sbk[0][:], kp2[hp * 64:(hp + 1) * 64, kt * 128:(kt + 1) * 128],
                     qp3[h // 2][hp * 64:(hp + 1) * 64, qb * 512:(qb + 1) * 512],
                     False, True, [Bkp3, Bqp3[h // 2]], [Sbk[1]])
                k.act(pt[0][:], Sbk[0][:], AF.Exp, [Sbk[1]], [pt[1]], scale=SC3)
                pend3[i] = pt

            def emit_pv3(i):
                h, qb, kt = steps[i]
                sl = h % 2
                if kt == 0:
                    cur["O"] = Ob3.next()
                    cur["D"] = Db3.next()
                O = cur["O"]
                Dn = cur["D"]
                pt = pend3.pop(i)
                k.mm(O[0][:], vb3[sl][:, kt, :], pt[0][:], kt == 0, kt == NT - 1, [Bhd[sl][2], pt[1]], [O[1]])
                k.mm(Dn[0][:], ones[:], pt[0][:], kt == 0, kt == NT - 1, [Bc, pt[1]], [Dn[1]])
                if kt == NT - 1:
                    rd = rdn3.next()
                    ob = ob3[qb % 2]
                    big_recip(rd[0][:], Dn[0][:], [Dn[1]], [rd[1]])
                    k.tt("dve", ob[0][:], O[0][:], rd[0][:], ALU.mult, [O[1], rd[1]], [ob[1]])
                    k.dma(oT_d[4 + h][:, qb * 512:(qb + 1) * 512], ob[0][:], [ob[1]], (), stck[6 + (qb % 2)])

            n3 = len(steps)
            load_head(1)
            SK3 = 2
            for i in range(n3 + SK3):
                if i < n3:
                    emit_s(i)
                    cur["hps"] = steps[i][0] % 2
                if i >= SK3:
                    emit_pv3(i - SK3)
                if i < n3 and i >= SK3:
                    h_, qb_, kt_ = steps[i - SK3 + 1]
                    if qb_ == 0 and kt_ == 0 and h_ + 1 < 4 and h_ >= 1:
                        load_head(h_ + 1)
            k.barrier()

    if 4 in phases or 5 in phases:
        p45 = ExitStack()
        x1T = sb(p45, "x1T", [128, 8, S + 2], BF16)
        Bx1T = k.bufs(NT, "x1T")
        Bpad = k.buf("x1Tpad")
        k.memset("pool", x1T[:, :, 0:1], 0.0, [Bpad])
        k.memset("pool", x1T[:, :, S + 1:S + 2], 0.0, [Bpad])
        wdn = sb(p45, "wdn", [128, NFF, D], BF16)
        Bwdn = k.buf("wdn")
        ckw = k.dclock("w45")
        ckwp = k.dclock("w45p")
        for j in range(NFF):
            k.dma(wdn[:, j, :], wdn_d[j * 128:(j + 1) * 128, :], (), [Bwdn], ckwp, q="pool")
        cwt = sb(p45, "cwt", [128, 3, 2 * NFF], F32)
        cbt = sb(p45, "cbt", [128, 2 * NFF], F32)
        Bcw = k.buf("convw")
        for t in range(3):
            k.op("sp", lambda t=t: nc.sync.dma_start(out=cwt[:, t, :], in_=cw_d[t].rearrange("(c p) -> p c", p=128),
                                                     allow_slow_non_contiguous=True), (), [Bcw], clock=ckw)
        k.op("sp", lambda: nc.sync.dma_start(out=cbt[:], in_=cb_d.rearrange("(c p) -> p c", p=128),
                                             allow_slow_non_contiguous=True), (), [Bcw], clock=ckw)
        NXB4 = 4
        xin4 = [sb(p45, "xin4_%d" % i, [128, D], F32) for i in range(NXB4)]
        Bxin4 = k.bufs(NXB4)
        lnsc4 = [ln_scratch(p45, "ln4_%d" % i) for i in range(4)]

        if 4 in phases:
            with ExitStack() as p4:
                ln1g = bcast_vec(p4, "ln1g", ln1g_d, D)
                ln1b = bcast_vec(p4, "ln1b", ln1b_d, D)
                wo = sb(p4, "wo", [128, 8, D], BF16)
                Bw4 = k.buf("w4")
                for c in range(8):
                    k.dma(wo[:, c, :], wo_d[c * 128:(c + 1) * 128, :], (), [Bw4], ckwp, q="pool")
                ong = sb(p4, "ong", [128, 8], F32)
                k.op("sp", lambda: nc.sync.dma_start(out=ong[:], in_=ong_d.rearrange("(c p) -> p c", p=128),
                                                     allow_slow_non_contiguous=True), (), [Bw4], clock=ckw)
                oTs = [sb(p4, "oTs%d" % i, [128, 8, 512], BF16) for i in range(2)]
                BoT = k.bufs(2)
                onb = [sb(p4, "onb%d" % i, [128, 8, 512], BF16) for i in range(2)]
                Bon = [k.bufs(8) for _ in range(2)]
                sqs = Rot([(sb(p4, "sq4_%d" % i, [128, 512], BF16), k.buf()) for i in range(3)])
                rr = [[sb(p4, "rr%d_%d" % (g, i), [128, 512], F32) for i in range(2)] for g in range(2)]
                Brr = [k.bufs(2) for _ in range(2)]
                x1b = [sb(p4, "x1b%d" % i, [128, D], BF16) for i in range(2)]
                Bx1b = k.bufs(2)
                ssq_ps = Rot([(ps(p4, "ssq%d" % i, [128, 512], F32), k.buf()) for i in range(2)])
                y_ps = Rot([(ps(p4, "y4_%d" % i, [128, D], F32), k.buf()) for i in range(2)])
                tp4 = [ps(p4, "tp4_%d" % i, [128, 8, 128], BF16) for i in range(2)]
                Btp4 = k.bufs(2)
                ck4 = [k.dclock("p4_%d" % i) for i in range(2)]

                def load_o(blk):
                    t0 = blk * 512
                    k.dma(oTs[blk % 2][:], oT_d[:, :, t0:t0 + 512].rearrange("c p t -> p c t"), (),
                          [BoT[blk % 2]], ck4[blk % 2])

                def blkgen(blk):
                    sl = blk % 2
                    if blk + 1 < 8:
                        load_o(blk + 1)
                    for g in range(2):
                        bank = ssq_ps.next()
                        for ci in range(4):
                            c = g * 4 + ci
                            sq = sqs.next()
                            k.act(sq[0][:], oTs[sl][:, c, :], AF.Square, [BoT[sl]], [sq[1]])
                            k.mm(bank[0][:], ones[:], sq[0][:], ci == 0, ci == 3, [Bc, sq[1]], [bank[1]])
                            yield
                        r = rr[g][sl]
                        big_rsqrt(r[:], bank[0][:], [bank[1]], [Brr[g][sl]], 1.0 / 512)
                        yield
                        for ci in range(4):
                            c = g * 4 + ci
                            k.stt("dve", onb[sl][:, c, :], oTs[sl][:, c, :], ong[:, c:c + 1], r[:], ALU.mult, ALU.mult,
                                  [BoT[sl], Brr[g][sl], Bw4], [Bon[sl][c]])
                            yield

                def tile4(blk, tt):
                    sl = blk % 2
                    ti = blk * 4 + tt
                    xi = ti % NXB4
                    li = ti % 4
                    k.dma(xin4[xi][:], xn_d[ti * 128:(ti + 1) * 128, :], (), [Bxin4[xi]], ldck[xi])
                    yield
                    y = y_ps.next()
                    for half in range(2):
                        for c in range(8):
                            k.mm(y[0][:, half * 512:(half + 1) * 512], onb[sl][:, c, tt * 128:(tt + 1) * 128],
                                 wo[:, c, half * 512:(half + 1) * 512], c == 0, c == 7, [Bon[sl][c], Bw4], [y[1]])
                    k.stt("dve", xin4[xi][:], xin4[xi][:], ALPHA, y[0][:], ALU.mult, ALU.add,
                          [Bxin4[xi], y[1]], [Bxin4[xi]])
                    yield
                    yield from layer_norm_g(lnsc4[li], xin4[xi][:], xin4[xi][:], ln1g, ln1b, Bxin4[xi], Bxin4[xi])
                    k.dma(x1_d[ti * 128:(ti + 1) * 128, :], xin4[xi][:], [Bxin4[xi]], (), stck[xi])
                    tp = li % 2
                    k.cp("act", x1b[tp][:], xin4[xi][:], [Bxin4[xi]], [Bx1b[tp]])
                    for c in range(8):
                        k.tr(tp4[tp][:, c, :], x1b[tp][:, c * 128:(c + 1) * 128], ident[:], [Bx1b[tp], Bc], [Btp4[tp]])
                    k.cp("act", x1T[:, :, 1 + ti * 128:1 + (ti + 1) * 128], tp4[tp][:], [Btp4[tp]], [Bx1T[ti]])
                    yield

                load_o(0)
                run_mix(blkgen(0), [], 1)
                tiles4 = [(blk, tt) for blk in range(8) for tt in range(4)]
                WIDTH4, STAG4 = 4, 1
                active4 = []
                nxt4 = 0
                bg = None
                bg_blk = 0
                bg_done = 0
                since = STAG4
                while nxt4 < len(tiles4) or active4 or bg is not None:
                    if (nxt4 < len(tiles4) and len(active4) < WIDTH4 and since >= STAG4
                            and tiles4[nxt4][0] <= bg_done):
                        blk, tt = tiles4[nxt4]
                        active4.append(tile4(blk, tt))
                        nxt4 += 1
                        since = 0
                        if tt == 0 and blk + 1 < 8:
                            assert bg is None
                            bg = blkgen(blk + 1)
                            bg_blk = blk + 1
                    since += 1
                    if bg is not None:
                        for _ in range(2):
                            try:
                                next(bg)
                            except StopIteration:
                                bg = None
                                bg_done = bg_blk
                                break
                    for g_ in list(active4):
                        try:
                            next(g_)
                        except StopIteration:
                            active4.remove(g_)
                k.barrier()

        if 5 in phases:
            with ExitStack() as p5:
                ln2g = bcast_vec(p5, "ln2g", ln2g_d, D)
                ln2b = bcast_vec(p5, "ln2b", ln2b_d, D)
                aT = sb(p5, "aT", [128, NFF, 1024], BF16)
                BaT = k.bufs(NFF)
                NWB = 2
                wgv = [sb(p5, "wgv%d" % i, [128, 2, 8, 128], BF16) for i in range(NWB)]
                Bwgv = k.bufs(NWB)
                ckg = [k.dclock("wg%d" % i) for i in range(NWB)]
                u_ps = [(ps(p5, "u%d" % i, [128, 1536], F32), k.buf()) for i in range(2)]
                y5 = (ps(p5, "y5", [128, D], F32), k.buf())
                accg = Rot([(sb(p5, "accg%d" % i, [128, 1024], F32), k.buf()) for i in range(2)])
                accv = Rot([(sb(p5, "accv%d" % i, [128, 1024], F32), k.buf()) for i in range(2)])
                wcount = [0]

                def load_w(j):
                    sl = wcount[0] % NWB
                    wcount[0] += 1
                    for g in range(2):
                        k.dma(wgv[sl][:, g, :, :], wupb_d[g * NFF + j], [Bwupb], [Bwgv[sl]], ckg[sl])
                    return sl

                order = [(qt, j) for qt in range(4) for j in range(NFF)]
                wslots = {}
                wslots[0] = load_w(order[0][1])
                for oi, (qt, j) in enumerate(order):
                    if oi + 1 < len(order):
                        wslots[oi + 1] = load_w(order[oi + 1][1])
                    wsl = wslots.pop(oi)
                    t0 = qt * 1024
                    accs = []
                    for g in range(2):
                        up = u_ps[g]
                        for (c0, c1) in ((0, 512), (512, 1024), (1024, 1026)):
                            for kc in range(8):
                                k.mm(up[0][:, c0:c1], wgv[wsl][:, g, kc, :],
                                     x1T[:, kc, t0 + c0:t0 + c1], kc == 0, kc == 7,
                                     [Bwgv[wsl], Bpad] + Bx1T[max(0, qt * 8 - 1):min(NT, qt * 8 + 9)], [up[1]])
                        ac = (accg if g == 0 else accv).next()
                        ch = g * NFF + j
                        k.act(ac[0][:], up[0][:, 1:1025], AF.Identity, [up[1], Bcw], [ac[1]],
                              bias=cbt[:, ch:ch + 1], scale=cwt[:, 1, ch:ch + 1])
                        k.stt("dve", ac[0][:], up[0][:, 0:1024], cwt[:, 0, ch:ch + 1], ac[0][:], ALU.mult, ALU.add,
                              [up[1], Bcw, ac[1]], [ac[1]])
                        k.stt("dve", ac[0][:], up[0][:, 2:1026], cwt[:, 2, ch:ch + 1], ac[0][:], ALU.mult, ALU.add,
                              [up[1], Bcw, ac[1]], [ac[1]])
                        accs.append(ac)
                    k.act(accs[0][0][:], accs[0][0][:], AF.Silu, [accs[0][1]], [accs[0][1]])
                    k.tt("pool", aT[:, j, :], accs[0][0][:], accs[1][0][:], ALU.mult, [accs[0][1], accs[1][1]], [BaT[j]])
                    if j == NFF - 1:
                        def tile5(tt, qt=qt):
                            ti = qt * 8 + tt
                            xi = ti % NXB4
                            li = ti % 4
                            k.dma(xin4[xi][:], x1_d[ti * 128:(ti + 1) * 128, :], (), [Bxin4[xi]], ldck[xi])
                            for half in range(2):
                                for jj in range(NFF):
                                    k.mm(y5[0][:, half * 512:(half + 1) * 512], aT[:, jj, tt * 128:(tt + 1) * 128],
                                         wdn[:, jj, half * 512:(half + 1) * 512], jj == 0, jj == NFF - 1,
                                         [BaT[jj], Bwdn], [y5[1]])
                            k.stt("dve", xin4[xi][:], xin4[xi][:], ALPHA, y5[0][:], ALU.mult, ALU.add,
                                  [Bxin4[xi], y5[1]], [Bxin4[xi]])
                            yield
                            yield from layer_norm_g(lnsc4[li], xin4[xi][:], xin4[xi][:], ln2g, ln2b, Bxin4[xi], Bxin4[xi])
                            k.dma(out_d[ti * 128:(ti + 1) * 128, :], xin4[xi][:], [Bxin4[xi]], (), stck[xi])
                            yield
                        run_mix(None, [tile5(tt) for tt in range(8)], 3)
                k.barrier()
        p45.close()
    k.barrier()
    return nc, es


def make_consts():
    ident = np.eye(128, dtype=np.float32).astype(ml_dtypes.bfloat16)
    ones = np.ones((128, 128), dtype=np.float32).astype(ml_dtypes.bfloat16)
    a = np.arange(128)[:, None]
    b = np.arange(256)[None, :]
    lq = b - 128
    lk = a - 64
    mask = (np.abs(lq - lk) <= 64).astype(np.float32).astype(ml_dtypes.bfloat16)
    rc = np.zeros((128, 8), dtype=np.float32)
    p = np.arange(128)
    d = p % 64
    invA = np.power(np.float32(THETA), -np.arange(8, dtype=np.float32) * np.float32(2.0 / 16)).astype(np.float32)
    rc[:, 0] = np.where(d < 16, invA[d % 8], 0.0)
    rc[:, 1] = np.where(d < 8, -1.0, 1.0)
    invB = np.power(np.float32(THETA), -np.arange(32, dtype=np.float32) * np.float32(2.0 / 64)).astype(np.float32)
    rc[:, 2] = invB[d % 32]
    rc[:, 3] = np.where(d < 32, -1.0, 1.0)
    return ident, ones, mask, rc


def kernel(**inputs):
    debug = os.environ.get("KDEBUG")
    import time as _t
    _tb = _t.time()
    phs = tuple(int(v) for v in os.environ.get("KPHASES", "1,2,3,4,5").split(","))
    nc, es = build_program(debug=debug, phases=phs)
    if debug:
        print("build time", _t.time() - _tb)
    ident, ones, mask, rc = make_consts()
    x = np.ascontiguousarray(inputs["x"], dtype=np.float32)
    pos = np.ascontiguousarray(inputs["positions"], dtype=np.int32)
    shared = {
        "ln_emb_g": inputs["ln_emb_g"], "ln_emb_b": inputs["ln_emb_b"],
        "w_in": inputs["w_in"][0], "q_norm_g": inputs["q_norm_g"][0], "w_uq": inputs["w_uq"][0],
        "kv_norm_g": inputs["kv_norm_g"][0], "w_ukv": inputs["w_ukv"][0],
        "out_norm_g": inputs["out_norm_g"][0], "w_o": inputs["w_o"][0],
        "ln1_g": inputs["ln1_g"][0], "ln1_b": inputs["ln1_b"][0], "w_up": inputs["w_up"][0],
        "conv_w": inputs["conv_w"][0], "conv_b": inputs["conv_b"][0], "w_down": inputs["w_down"][0],
        "ln2_g": inputs["ln2_g"][0], "ln2_b": inputs["ln2_b"][0],
        "c_ident": ident, "c_ones": ones, "c_mask": mask, "c_rope": rc,
    }
    shared = {kk: np.ascontiguousarray(v) for kk, v in shared.items()}
    in_maps = []
    for b in range(8):
        m = dict(shared)
        m["x"] = x[b]
        m["pos"] = pos[b]
        in_maps.append(m)
    import time as _t
    _t0 = _t.time()
    ncores = int(os.environ.get("KCORES", "8"))
    res = run_bass_kernel_spmd(nc, in_maps[:ncores], core_ids=list(range(ncores)))
    if debug:
        print("run time", _t.time() - _t0)
    es.close()
    if debug:
        return res
    return np.stack([np.asarray(r["out"], dtype=np.float32) for r in res.results], axis=0)
```

```python
import math
import os
from contextlib import ExitStack

import numpy as np
import ml_dtypes
import concourse.bass as bass
import concourse.mybir as mybir
from concourse.bass_utils import run_bass_kernel_spmd

F32 = mybir.dt.float32
BF16 = mybir.dt.bfloat16
I32 = mybir.dt.int32
AF = mybir.ActivationFunctionType
ALU = mybir.AluOpType

S = 4096
D = 1024
NT = S // 128
DFF = 2816
NFF = DFF // 128
LN_EPS = 1e-5
RMS_EPS = 1e-6
ALPHA = 2.0 ** 0.25
THETA = 500000.0
PI = math.pi
MAGIC = 12582912.0
C1 = 6.28125
C2 = 2.0 * PI - C1
PATTERNS = (1, 4, 16)
KPAD = 1024


class Clock:
    def __init__(self, sem, name):
        self.sem = sem
        self.val = 0
        self.name = name
        self.wp = []
        self.rp = []


class Buf:
    __slots__ = ("name", "w", "r")

    def __init__(self, name):
        self.name = name
        self.w = None
        self.r = {}


class Eng:
    def __init__(self, name, eng, clock):
        self.name = name
        self.eng = eng
        self.clock = clock
        self.seen = {}


class Kb:
    def __init__(self, nc, es):
        self.nc = nc
        self.es = es
        self.E = {}
        for name, eng in (("pe", nc.tensor), ("act", nc.scalar), ("dve", nc.vector),
                          ("pool", nc.gpsimd), ("sp", nc.sync)):
            sem = es.enter_context(nc.semaphore("s_" + name))
            self.E[name] = Eng(name, eng, Clock(sem, name))
        self.clocks = [e.clock for e in self.E.values()]
        self.nbuf = 0

    def dclock(self, name):
        sem = self.es.enter_context(self.nc.semaphore("d_" + name))
        c = Clock(sem, name)
        self.clocks.append(c)
        return c

    def buf(self, name=None):
        self.nbuf += 1
        return Buf(name or "b%d" % self.nbuf)

    def bufs(self, n, name=None):
        return [self.buf(name) for _ in range(n)]

    def _wait(self, e, c, v):
        if e.seen.get(c, 0) >= v:
            return
        e.eng.wait_ge(c.sem, v)
        e.seen[c] = v

    def op(self, en, fn, R=(), W=(), clock=None):
        e = self.E[en]
        need = {}

        def add(c, v):
            if e.seen.get(c, 0) >= v:
                return
            if need.get(c, 0) < v:
                need[c] = v

        for b in R:
            if b.w is not None:
                c, v = b.w
                if not (c is e.clock and en == "pe"):
                    add(c, v)
        for b in W:
            if b.w is not None:
                c, v = b.w
                if not (c is e.clock and en == "pe"):
                    add(c, v)
            for c, v in b.r.items():
                if c is e.clock:
                    continue
                add(c, v)
        items = list(need.items())
        embed = None
        if items and en != "pe":
            embed = items.pop()
        for c, v in items:
            self._wait(e, c, v)
        ins = fn()
        if embed is not None:
            ins._wait_ge(embed[0].sem, embed[1])
            e.seen[embed[0]] = embed[1]
        if clock is None:
            ck = e.clock
            step = 1
        else:
            ck = clock
            step = 16
        ins.then_inc(ck.sem, step)
        ck.val += step
        for b in R:
            if b.r.get(ck, 0) < ck.val:
                b.r[ck] = ck.val
        for b in W:
            b.w = (ck, ck.val)
            b.r = {}
        if clock is not None:
            ck.wp = [b for b in ck.wp if b.w is not None and b.w[0] is ck]
            for b in ck.wp:
                b.w = (ck, ck.val)
            ck.rp = [b for b in ck.rp if ck in b.r]
            for b in ck.rp:
                b.r[ck] = ck.val
            for b in W:
                if b not in ck.wp:
                    ck.wp.append(b)
            for b in R:
                if b not in ck.rp:
                    ck.rp.append(b)
        return ins

    def barrier(self):
        for e in self.E.values():
            for c in self.clocks:
                if c is e.clock:
                    continue
                if c.val > 0:
                    self._wait(e, c, c.val)

    def mm(self, out, lhsT, rhs, start, stop, R, W, **kw):
        nc = self.nc
        return self.op("pe", lambda: nc.tensor.matmul(out, lhsT, rhs, start=start, stop=stop, **kw), R, W)

    def tr(self, out, in_, ident, R, W):
        nc = self.nc
        return self.op("pe", lambda: nc.tensor.transpose(out, in_, ident), R, W)

    def act(self, out, in_, func, R, W, **kw):
        nc = self.nc
        return self.op("act", lambda: nc.scalar.activation(out, in_, func, **kw), R, W)

    def dma(self, out, in_, R, W, clock, q="sp"):
        e = self.E[q]
        return self.op(q, lambda: e.eng.dma_start(out=out, in_=in_), R, W, clock=clock)

    def tt(self, en, out, a, b, op, R, W):
        e = self.E[en]
        return self.op(en, lambda: e.eng.tensor_tensor(out, a, b, op), R, W)

    def ts(self, en, out, a, s1, s2, op0, op1, R, W):
        e = self.E[en]
        if op1 is None:
            return self.op(en, lambda: e.eng.tensor_scalar(out, a, s1, None, op0), R, W)
        return self.op(en, lambda: e.eng.tensor_scalar(out, a, s1, s2, op0, op1), R, W)

    def stt(self, en, out, a, s, b, op0, op1, R, W):
        e = self.E[en]
        return self.op(en, lambda: e.eng.scalar_tensor_tensor(out, a, s, b, op0, op1), R, W)

    def cp(self, en, out, in_, R, W):
        e = self.E[en]
        if en == "act":
            return self.op(en, lambda: e.eng.copy(out, in_), R, W)
        return self.op(en, lambda: e.eng.tensor_copy(out, in_), R, W)

    def memset(self, en, ap, val, W):
        e = self.E[en]
        return self.op(en, lambda: e.eng.memset(ap, val), (), W)


class Rot:
    def __init__(self, items):
        self.items = items
        self.i = 0

    def next(self):
        it = self.items[self.i % len(self.items)]
        self.i += 1
        return it


def interleave(*lists):
    out = []
    n = max(len(l) for l in lists)
    for i in range(n):
        for l in lists:
            k0 = (i * len(l)) // n
            k1 = ((i + 1) * len(l)) // n
            out.extend(l[k0:k1])
    return out


def build_program(debug=None, phases=(1, 2, 3, 4, 5)):
    nc = bass.Bass("TRN2", target_bir_lowering=False)
    es = ExitStack()
    dbg_kind = "ExternalOutput" if debug else "Internal"

    def din(name, shape, dt):
        return nc.dram_tensor(name, list(shape), dt, kind="ExternalInput").ap()

    def dscr(name, shape, dt):
        return nc.dram_tensor(name, list(shape), dt, kind=dbg_kind).ap()

    x_d = din("x", [S, D], F32)
    pos_d = din("pos", [S], I32)
    lneg_d = din("ln_emb_g", [D], F32)
    lneb_d = din("ln_emb_b", [D], F32)
    win_d = din("w_in", [D, 1984], F32)
    qng_d = din("q_norm_g", [256], F32)
    wuq_d = din("w_uq", [256, 768], F32)
    kvng_d = din("kv_norm_g", [128], F32)
    wukv_d = din("w_ukv", [128, 1024], F32)
    ong_d = din("out_norm_g", [D], F32)
    wo_d = din("w_o", [D, D], F32)
    ln1g_d = din("ln1_g", [D], F32)
    ln1b_d = din("ln1_b", [D], F32)
    wup_d = din("w_up", [D, 2 * DFF], F32)
    cw_d = din("conv_w", [3, 2 * DFF], F32)
    cb_d = din("conv_b", [2 * DFF], F32)
    wdn_d = din("w_down", [DFF, D], F32)
    ln2g_d = din("ln2_g", [D], F32)
    ln2b_d = din("ln2_b", [D], F32)
    ident_d = din("c_ident", [128, 128], BF16)
    ones_d = din("c_ones", [128, 128], BF16)
    mask_d = din("c_mask", [128, 256], BF16)
    rc_d = din("c_rope", [128, 8], F32)
    out_d = nc.dram_tensor("out", [S, D], F32, kind="ExternalOutput").ap()

    qkT_d = dscr("s_qkT", [8, 128, S], BF16)
    va_d = dscr("s_va", [S, 8, 65], BF16)
    qnT_d = dscr("s_qnT", [4, 128, S], BF16)
    qpT_d = dscr("s_qpT", [2, 128, S], BF16)
    knT_d = dscr("s_knT", [4, 128, S], BF16)
    kpT_d = dscr("s_kpT", [64, S], BF16)
    vb_d = dscr("s_vb", [S, 512], BF16)
    x1_d = dscr("s_x1", [S, D], F32)
    oT_d = dscr("s_oT", [8, 128, S], BF16)
    dbg_xn = dscr("dbg_xn", [S, D], BF16) if debug else None
    xn_d = dscr("s_xn", [S, D], F32)
    wupb_d = dscr("s_wupb", [2 * NFF, 128, 8, 128], BF16)

    k = Kb(nc, es)
    Bwupb = k.buf("wupb")
    ckwup = k.dclock("wup")

    def sb(stack, name, shape, dt):
        return stack.enter_context(nc.sbuf_tensor(name, list(shape), dt))

    def ps(stack, name, shape, dt):
        return stack.enter_context(nc.psum_tensor(name, list(shape), dt))

    ident = sb(es, "ident", [128, 128], BF16)
    ones = sb(es, "ones", [128, 128], BF16)
    mask = sb(es, "mask", [128, 256], BF16)
    rc = sb(es, "rc", [128, 8], F32)
    Bc = k.buf("consts")
    ck0 = k.dclock("c0")
    k.dma(ident[:], ident_d, (), [Bc], ck0)
    k.dma(ones[:], ones_d, (), [Bc], ck0)
    k.dma(mask[:], mask_d, (), [Bc], ck0)
    k.dma(rc[:], rc_d, (), [Bc], ck0)
    epsln = sb(es, "epsln", [128, 1], F32)
    epsrms = sb(es, "epsrms", [128, 1], F32)
    halfpi = sb(es, "halfpi", [128, 1], F32)
    k.memset("dve", epsln[:], LN_EPS, [Bc])
    k.memset("dve", epsrms[:], RMS_EPS, [Bc])
    k.memset("dve", halfpi[:], PI / 2, [Bc])
    k.memset("dve", halfpi[:], PI / 2, [Bc])

    ldck = [k.dclock("ld%d" % i) for i in range(8)]
    stck = [k.dclock("st%d" % i) for i in range(8)]

    def layer_norm_g(stack_tiles, src, dst, g_t, b_t, Bsrc, Bdst, out_bf=None, Bbf=None):
        st, mv, rs, nmr = stack_tiles
        Bs = stack_tiles_buf[id(st)]
        for hh in range(2):
            k.op("dve", lambda hh=hh: nc.vector.bn_stats(st[:, hh, :], src[:, hh * 512:(hh + 1) * 512]),
                 [Bsrc], [Bs])
        yield
        k.op("dve", lambda: nc.vector.bn_aggr(mv[:], st[:].rearrange("p a b -> p (a b)")), [Bs], [Bs])
        yield
        k.act(rs[:], mv[:, 1:2], AF.Ln, [Bs, Bc], [Bs], bias=epsln[:], scale=1.0)
        k.stt("dve", dst, src, mv[:, 0:1], g_t[:], ALU.subtract, ALU.mult, [Bsrc, Bs, Bc], [Bdst])
        yield
        k.act(rs[:], rs[:], AF.Exp, [Bs], [Bs], scale=-0.5)
        yield
        if out_bf is None:
            k.stt("dve", dst, dst, rs[:], b_t[:], ALU.mult, ALU.add, [Bdst, Bs, Bc], [Bdst])
        else:
            k.stt("dve", out_bf, dst, rs[:], b_t[:], ALU.mult, ALU.add, [Bdst, Bs, Bc], [Bbf])
        yield

    def run_mix(main, sides, width):
        sides = list(sides)
        active = []
        while main is not None or active or sides:
            while len(active) < width and sides:
                active.append(sides.pop(0))
            if main is not None:
                try:
                    next(main)
                except StopIteration:
                    main = None
            for g in list(active):
                try:
                    next(g)
                except StopIteration:
                    active.remove(g)

    stack_tiles_buf = {}

    def big_rsqrt(out, in_, R, W, scale):
        k.act(out, in_, AF.Ln, list(R) + [Bc], W, bias=epsrms[:], scale=scale)
        k.act(out, out, AF.Exp, W, W, scale=-0.5)

    def big_recip(out, in_, R, W):
        k.act(out, in_, AF.Ln, R, W)
        k.act(out, out, AF.Exp, W, W, scale=-1.0)

    def ln_scratch(stack, name):
        st = sb(stack, name + "_st", [128, 2, 6], F32)
        mv = sb(stack, name + "_mv", [128, 2], F32)
        rs = sb(stack, name + "_rs", [128, 1], F32)
        nmr = sb(stack, name + "_nm", [128, 1], F32)
        stack_tiles_buf[id(st)] = k.buf(name)
        return (st, mv, rs, nmr)

    def bcast_vec(stack, name, src_d, n):
        t = sb(stack, name, [128, n], F32)
        k.dma(t[:], src_d.partition_broadcast(128), (), [Bc], ck0)
        return t


    if 1 in phases:
        with ExitStack() as p1:
            lneg = bcast_vec(p1, "lneg", lneg_d, D)
            lneb = bcast_vec(p1, "lneb", lneb_d, D)
            win = sb(p1, "win", [128, 8, 1984], BF16)
            winp = sb(p1, "winp", [128, 8, 1024], BF16)
            winpk = sb(p1, "winpk", [128, 8, 64], BF16)
            Bw = k.buf("w1")
            ck0p = k.dclock("c0p")
            for kc in range(8):
                k.dma(win[:, kc, :], win_d[kc * 128:(kc + 1) * 128, :], (), [Bw], ck0p, q="pool")
            k.memset("pool", winp[:], 0.0, [Bw])
            wv = win[:, :, 0:1024].rearrange("p k (h d) -> p k h d", d=64)
            wpv = winp[:].rearrange("p k (h d) -> p k h d", d=64)
            for kc in range(8):
                k.cp("pool", wpv[:, kc, :, 0:8], wv[:, kc, :, 8:16], [Bw], [Bw])
                k.cp("pool", wpv[:, kc, :, 8:16], wv[:, kc, :, 0:8], [Bw], [Bw])
            k.cp("pool", winpk[:, :, 0:32], win[:, :, 1952:1984], [Bw], [Bw])
            k.cp("pool", winpk[:, :, 32:64], win[:, :, 1920:1952], [Bw], [Bw])
            qng = sb(p1, "qng", [128, 2], F32)
            wuq = sb(p1, "wuq", [128, 2, 768], BF16)
            wqn = sb(p1, "wqn", [128, 2, 4, 128], BF16)
            wqp = sb(p1, "wqp", [128, 2, 4, 64], BF16)
            wqpp = sb(p1, "wqpp", [128, 2, 4, 64], BF16)
            kvng = sb(p1, "kvng", [128, 1], F32)
            wkn = sb(p1, "wkn", [128, 4, 128], BF16)
            wvb = sb(p1, "wvb", [128, 4, 128], BF16)
            CA = sb(p1, "CA", [128, S], BF16)
            SA = sb(p1, "SA", [128, S], BF16)
            CB = sb(p1, "CB", [128, S], BF16)
            SB_ = sb(p1, "SB", [128, S], BF16)
            Btab = k.buf("tab")
            pw = ExitStack()
            wuq_f = sb(pw, "wuq_f", [128, 2, 768], F32)
            wukv_f = sb(pw, "wukv_f", [128, 1024], F32)
            posi = sb(pw, "posi", [128, S], I32)
            posf = sb(pw, "posf", [128, S], F32)
            ang = sb(pw, "ang", [128, S], F32)
            kk = sb(pw, "kk", [128, S], F32)
            Bt = k.buf("tabtmp")
            k.dma(posi[:], pos_d.partition_broadcast(128), (), [Bt], ck0)
            k.dma(wuq_f[:], wuq_d.rearrange("(k p) n -> p k n", p=128), (), [Bw], ck0)
            for kc in range(2):
                k.dma(qng[:, kc:kc + 1], qng_d[kc * 128:(kc + 1) * 128].rearrange("(p o) -> p o", o=1), (), [Bw], ck0)
            k.dma(wukv_f[:], wukv_d, (), [Bw], ck0)
            k.dma(kvng[:], kvng_d.rearrange("(p o) -> p o", o=1), (), [Bw], ck0)
            k.cp("dve", posf[:], posi[:], [Bt], [Bt])
            for (fc, sc, Ct, St) in ((0, 1, CA, SA), (2, 3, CB, SB_)):
                k.ts("dve", ang[:], posf[:], rc[:, fc:fc + 1], None, ALU.mult, None, [Bt, Bc], [Bt])
                k.ts("dve", kk[:], ang[:], 1.0 / (2 * PI), MAGIC, ALU.mult, ALU.add, [Bt], [Bt])
                k.ts("dve", kk[:], kk[:], MAGIC, None, ALU.subtract, None, [Bt], [Bt])
                k.stt("dve", ang[:], kk[:], -C1, ang[:], ALU.mult, ALU.add, [Bt], [Bt])
                k.stt("dve", ang[:], kk[:], -C2, ang[:], ALU.mult, ALU.add, [Bt], [Bt])
                k.ts("dve", ang[:], ang[:], PI, -PI, ALU.min, ALU.max, [Bt], [Bt])
                k.act(St[:], ang[:], AF.Sin, [Bt, Bc], [Btab], scale=rc[:, sc:sc + 1])
                k.act(kk[:], ang[:], AF.Abs, [Bt], [Bt])
                k.act(Ct[:], kk[:], AF.Sin, [Bt, Bc], [Btab], scale=-1.0, bias=halfpi[:])
            for kc in range(2):
                k.ts("dve", wuq[:, kc, :], wuq_f[:, kc, :], qng[:, kc:kc + 1], None, ALU.mult, None, [Bw], [Bw])
            wuq4 = wuq[:].rearrange("p k (h d) -> p k h d", d=192)
            for kc in range(2):
                k.cp("dve", wqn[:, kc, :, :], wuq4[:, kc, :, 0:128], [Bw], [Bw])
                k.cp("dve", wqp[:, kc, :, :], wuq4[:, kc, :, 128:192], [Bw], [Bw])
                k.cp("dve", wqpp[:, kc, :, 0:32], wuq4[:, kc, :, 160:192], [Bw], [Bw])
                k.cp("dve", wqpp[:, kc, :, 32:64], wuq4[:, kc, :, 128:160], [Bw], [Bw])
            wukv4 = wukv_f[:].rearrange("p (h d) -> p h d", d=256)
            k.ts("dve", wkn[:], wukv4[:, :, 0:128], kvng[:, 0:1], None, ALU.mult, None, [Bw], [Bw])
            k.ts("dve", wvb[:], wukv4[:, :, 128:256], kvng[:, 0:1], None, ALU.mult, None, [Bw], [Bw])
            k.barrier()
            pw.close()

            NXB = 2
            xin = [sb(p1, "xin%d" % i, [128, D], F32) for i in range(NXB)]
            Bxin = k.bufs(NXB, "xin")
            xnb = [sb(p1, "xnb%d" % i, [128, D], BF16) for i in range(2)]
            Bxnb = k.bufs(2, "xnb")
            lnsc = [ln_scratch(p1, "ln%d" % i) for i in range(2)]
            xnT = [sb(p1, "xnT%d" % i, [128, 8, 512], BF16) for i in range(2)]
            BxnT = [k.bufs(4, "xnT") for _ in range(2)]
            tp_ps = [ps(p1, "tp%d" % i, [128, 8, 128], BF16) for i in range(2)]
            Btp = k.bufs(2, "tp")
            NB = 6
            banks = Rot([(ps(p1, "bk%d" % i, [128, 512], F32), k.buf("bk%d" % i)) for i in range(NB)])
            qk_st = [sb(p1, "qk_st%d" % i, [128, 8, 512], BF16) for i in range(2)]
            qn_st = [sb(p1, "qn_st%d" % i, [128, 4, 512], BF16) for i in range(2)]
            qp_st = [sb(p1, "qp_st%d" % i, [128, 2, 512], BF16) for i in range(2)]
            kn_st = [sb(p1, "kn_st%d" % i, [128, 4, 512], BF16) for i in range(2)]
            kp_st = [sb(p1, "kp_st%d" % i, [64, 512], BF16) for i in range(2)]
            va_st = [sb(p1, "va_st%d" % i, [128, 4, 8, 65], BF16) for i in range(2)]
            vb_st = [sb(p1, "vb_st%d" % i, [128, 4, 512], BF16) for i in range(2)]
            Bqk = [k.bufs(8) for _ in range(2)]
            Bqn = [k.bufs(4) for _ in range(2)]
            Bqp = [k.bufs(2) for _ in range(2)]
            Bkn = [k.bufs(4) for _ in range(2)]
            Bkp = k.bufs(2)
            Bva = [k.bufs(4) for _ in range(2)]
            Bvb = [k.bufs(4) for _ in range(2)]
            for i in range(2):
                k.memset("pool", va_st[i][:, :, :, 64:65], 1.0, Bva[i])
            cq_bf = [sb(p1, "cq%d" % i, [128, 2, 512], BF16) for i in range(1)]
            ckv_bf = [sb(p1, "ckv%d" % i, [128, 512], BF16) for i in range(1)]
            sq_q = [sb(p1, "sqq%d" % i, [128, 2, 512], BF16) for i in range(1)]
            sq_kv = [sb(p1, "sqkv%d" % i, [128, 512], BF16) for i in range(1)]
            Bcq = [k.bufs(2) for _ in range(1)]
            Bckv = k.bufs(1)
            Bsqq = [k.bufs(2) for _ in range(1)]
            Bsqkv = k.bufs(1)
            rq = [sb(p1, "rq%d" % i, [128, 512], F32) for i in range(1)]
            rkv = [sb(p1, "rkv%d" % i, [128, 512], F32) for i in range(1)]
            Brq = k.bufs(1)
            Brkv = k.bufs(1)
            rkvc = [sb(p1, "rkvc%d" % i, [128, 4], F32) for i in range(1)]
            Brkvc = [k.bufs(4) for _ in range(1)]
            NTMP = 2
            tmpA = Rot([(sb(p1, "tmpA%d" % i, [128, 512], F32), k.buf()) for i in range(NTMP)])
            tmpB = Rot([(sb(p1, "tmpB%d" % i, [128, 512], F32), k.buf()) for i in range(NTMP)])

            def stageA(blk):
                gens = []
                sl = blk % 2
                for tt in range(4):
                    def g(tt=tt):
                        ti = blk * 4 + tt
                        xi = ti % NXB
                        li = ti % 2
                        k.dma(xin[xi][:], x_d[ti * 128:(ti + 1) * 128, :], (), [Bxin[xi]], ldck[xi])
                        yield
                        yield from layer_norm_g(lnsc[li], xin[xi][:], xin[xi][:], lneg, lneb, Bxin[xi], Bxin[xi])
                        k.dma(xn_d[ti * 128:(ti + 1) * 128, :], xin[xi][:], [Bxin[xi]], (), stck[2 + xi])
                        k.cp("act", xnb[li][:], xin[xi][:], [Bxin[xi]], [Bxnb[li]])
                        yield
                        if debug:
                            k.dma(dbg_xn[ti * 128:(ti + 1) * 128, :], xnb[li][:], [Bxnb[li]], (), stck[4 + li])
                        for kc in range(8):
                            k.tr(tp_ps[li][:, kc, :], xnb[li][:, kc * 128:(kc + 1) * 128], ident[:],
                                 [Bxnb[li], Bc], [Btp[li]])
                        yield
                        k.cp("act", xnT[sl][:, :, tt * 128:(tt + 1) * 128], tp_ps[li][:], [Btp[li]], [BxnT[sl][tt]])
                        yield
                    gens.append(g())
                return gens

            def rope_combine(hA, BA, hB, BB, Ct, St, t0, out, Bout, npart=128, mul=None, Bmul=None):
                tA, BtA = tmpA.next()
                tB, BtB = tmpB.next()
                k.tt("dve", tA[0:npart, :], hA, Ct[0:npart, t0:t0 + 512], ALU.mult, [BA, Btab], [BtA])
                k.tt("dve", tB[0:npart, :], hB, St[0:npart, t0:t0 + 512], ALU.mult, [BB, Btab], [BtB])
                if mul is None:
                    k.tt("pool", out, tA[0:npart, :], tB[0:npart, :], ALU.add, [BtA, BtB], [Bout])
                else:
                    k.tt("pool", tA[0:npart, :], tA[0:npart, :], tB[0:npart, :], ALU.add, [BtA, BtB], [BtA])
                    k.tt("pool", out, tA[0:npart, :], mul, ALU.mult, [BtA, Bmul], [Bout])

            def stageB(blk):
                items = []
                sl = blk % 2
                t0 = blk * 512
                RX = BxnT[sl]

                def proj(wt, c0, m, bank):
                    for kc in range(8):
                        k.mm(bank[0][0:m, :], wt[:, kc, c0:c0 + m], xnT[sl][:, kc, :], kc == 0, kc == 7,
                             RX + [Bw], [bank[1]])

                for c in range(8):
                    def f(c=c):
                        bA = banks.next()
                        bB = banks.next()
                        proj(win, c * 128, 128, bA)
                        proj(winp, c * 128, 128, bB)
                        rope_combine(bA[0][:], bA[1], bB[0][:], bB[1], CA, SA, t0, qk_st[sl][:, c, :], Bqk[sl][c])
                    items.append(f)
                for i in range(2):
                    def f(i=i):
                        b = banks.next()
                        proj(win, 1536 + i * 128, 128, b)
                        k.cp("act", cq_bf[0][:, i, :], b[0][:], [b[1]], [Bcq[0][i]])
                        k.act(sq_q[0][:, i, :], b[0][:], AF.Square, [b[1]], [Bsqq[0][i]])
                    items.append(f)

                def f():
                    b = banks.next()
                    proj(win, 1792, 128, b)
                    k.cp("act", ckv_bf[0][:], b[0][:], [b[1]], [Bckv[0]])
                    k.act(sq_kv[0][:], b[0][:], AF.Square, [b[1]], [Bsqkv[0]])
                items.append(f)

                def f():
                    bA = banks.next()
                    bB = banks.next()
                    proj(win, 1920, 64, bA)
                    for kc in range(8):
                        k.mm(bB[0][0:64, :], winpk[:, kc, :], xnT[sl][:, kc, :], kc == 0, kc == 7, RX + [Bw], [bB[1]])
                    rope_combine(bA[0][0:64, :], bA[1], bB[0][0:64, :], bB[1], CB, SB_, t0, kp_st[sl][:], Bkp[sl],
                                 npart=64)
                items.append(f)

                for tt in range(4):
                    def f(tt=tt):
                        b = banks.next()
                        for kc in range(8):
                            k.mm(b[0][:], xnT[sl][:, kc, tt * 128:(tt + 1) * 128], win[:, kc, 1024:1536],
                                 kc == 0, kc == 7, [RX[tt], Bw], [b[1]])
                        k.cp("act", va_st[sl][:, tt, :, 0:64], b[0][:].rearrange("p (h d) -> p h d", d=64),
                             [b[1]], [Bva[sl][tt]])
                    items.append(f)

                def f():
                    b = banks.next()
                    for i in range(2):
                        k.mm(b[0][:], ones[:], sq_q[0][:, i, :], i == 0, i == 1, [Bsqq[0][i], Bc], [b[1]])
                    big_rsqrt(rq[0][:], b[0][:], [b[1]], [Brq[0]], 1.0 / 256)
                    b2 = banks.next()
                    k.mm(b2[0][:], ones[:], sq_kv[0][:], True, True, [Bsqkv[0], Bc], [b2[1]])
                    big_rsqrt(rkv[0][:], b2[0][:], [b2[1]], [Brkv[0]], 1.0 / 128)
                items.append(f)

                for h in range(4):
                    def f(h=h):
                        b = banks.next()
                        for kc in range(2):
                            k.mm(b[0][:], wqn[:, kc, h, :], cq_bf[0][:, kc, :], kc == 0, kc == 1,
                                 [Bcq[0][kc], Bw], [b[1]])
                        k.tt("dve", qn_st[sl][:, h, :], b[0][:], rq[0][:], ALU.mult, [b[1], Brq[0]], [Bqn[sl][h]])
                    items.append(f)
                for i in range(2):
                    def f(i=i):
                        bA = banks.next()
                        bB = banks.next()
                        for kc in range(2):
                            k.mm(bA[0][:], wqp[:, kc, 2 * i:2 * i + 2, :], cq_bf[0][:, kc, :], kc == 0, kc == 1,
                                 [Bcq[0][kc], Bw], [bA[1]])
                        for kc in range(2):
                            k.mm(bB[0][:], wqpp[:, kc, 2 * i:2 * i + 2, :], cq_bf[0][:, kc, :], kc == 0, kc == 1,
                                 [Bcq[0][kc], Bw], [bB[1]])
                        rope_combine(bA[0][:], bA[1], bB[0][:], bB[1], CB, SB_, t0, qp_st[sl][:, i, :], Bqp[sl][i],
                                     mul=rq[0][:], Bmul=Brq[0])
                    items.append(f)
                for h in range(4):
                    def f(h=h):
                        b = banks.next()
                        k.mm(b[0][:], wkn[:, h, :], ckv_bf[0][:], True, True, [Bckv[0], Bw], [b[1]])
                        k.tt("dve", kn_st[sl][:, h, :], b[0][:], rkv[0][:], ALU.mult, [b[1], Brkv[0]], [Bkn[sl][h]])
                    items.append(f)
                for tt in range(4):
                    def f(tt=tt):
                        b2 = banks.next()
                        k.mm(b2[0][:, 0:2], sq_kv[0][:, tt * 128:(tt + 1) * 128], ones[:, 0:2], True, True,
                             [Bsqkv[0], Bc], [b2[1]])
                        col = rkvc[0][:, tt:tt + 1]
                        big_rsqrt(col, b2[0][:, 0:1], [b2[1]], [Brkvc[0][tt]], 1.0 / 128)
                        b = banks.next()
                        k.mm(b[0][:], ckv_bf[0][:, tt * 128:(tt + 1) * 128], wvb[:].rearrange("p h d -> p (h d)"),
                             True, True, [Bckv[0], Bw], [b[1]])
                        k.act(vb_st[sl][:, tt, :], b[0][:], AF.Copy, [b[1], Brkvc[0][tt]], [Bvb[sl][tt]], scale=col)
                    items.append(f)
                return items

            def stores(blk):
                sl = blk % 2
                t0 = blk * 512
                cs = stck[sl]
                k.dma(qkT_d[:, :, t0:t0 + 512].rearrange("c p t -> p c t"), qk_st[sl][:], Bqk[sl], (), cs)
                k.dma(qnT_d[:, :, t0:t0 + 512].rearrange("c p t -> p c t"), qn_st[sl][:], Bqn[sl], (), cs)
                k.dma(qpT_d[:, :, t0:t0 + 512].rearrange("c p t -> p c t"), qp_st[sl][:], Bqp[sl], (), cs)
                k.dma(knT_d[:, :, t0:t0 + 512].rearrange("c p t -> p c t"), kn_st[sl][:], Bkn[sl], (), cs)
                k.dma(kpT_d[:, t0:t0 + 512], kp_st[sl][:], [Bkp[sl]], (), cs)
                k.dma(va_d[t0:t0 + 512].rearrange("(a p) h d -> p a h d", p=128), va_st[sl][:], Bva[sl], (), cs)
                k.dma(vb_d[t0:t0 + 512].rearrange("(a p) n -> p a n", p=128), vb_st[sl][:], Bvb[sl], (), cs)

            def genB(blk):
                for f in stageB(blk):
                    f()
                    yield

            run_mix(None, stageA(0), 2)
            for blk in range(8):
                A = stageA(blk + 1) if blk < 7 else []
                run_mix(genB(blk), A, 2)
                stores(blk)
            k.barrier()

    if 2 in phases:
        with ExitStack() as p2:
            SC2 = 0.125
            mask2 = sb(p2, "mask2", [128, 2, 256], BF16)
            Bm2 = k.buf("mask2")
            for hp in range(2):
                k.cp("pool", mask2[:, hp, :], mask[:], [Bc], [Bm2])
            onesf = sb(p2, "onesf", [128, 64], F32)
            k.memset("pool", onesf[:], 1.0, [Bm2])
            qc = [sb(p2, "a_q%d" % i, [128, S], BF16) for i in range(2)]
            kc_ = [sb(p2, "a_k%d" % i, [128, S + 2 * KPAD], BF16) for i in range(2)]
            Bqc = k.bufs(2)
            Bkc = k.bufs(2)
            for i in range(2):
                k.memset("pool", kc_[i][:, 0:KPAD], 0.0, [Bkc[i]])
                k.memset("pool", kc_[i][:, KPAD + S:], 0.0, [Bkc[i]])
            NTMAX = 48
            Vt = [sb(p2, "a_v%d" % i, [128, NTMAX, 130], BF16) for i in range(3)]
            BVt = k.bufs(3)
            Oaccs = [sb(p2, "a_oacc%d" % i, [65, 2, S], F32) for i in range(2)]
            BOaccs = [[[k.buf() for _ in range(16)] for _ in range(NT)] for _ in range(2)]
            BOacc = BOaccs[0]

            def oacc_bufs(t0, d, nq, BOacc):
                blocks = range(t0 // 128, (t0 + (nq - 1) * d) // 128 + 1)
                res = sorted(set((t0 + b * d) % 16 for b in range(16)))
                return [BOacc[bl][rr_] for bl in blocks for rr_ in res]
            S2 = [(ps(p2, "a_S%d" % i, [128, 2, 512], F32), k.buf()) for i in range(2)]
            Obk = Rot([(ps(p2, "a_O%d" % i, [128, 2, 256], F32), k.buf()) for i in range(3)])
            finb = (ps(p2, "a_fin", [128, 512], F32), k.buf())
            PT2 = Rot([(sb(p2, "a_PT%d" % i, [128, 2, 256], BF16), k.buf()) for i in range(8)])
            rdn2 = Rot([(sb(p2, "a_rd%d" % i, [64, 512], F32), k.buf()) for i in range(2)])
            oa_st = Rot([(sb(p2, "a_ost%d" % i, [64, 512], BF16), k.buf()) for i in range(2)])
            cka = [k.dclock("a%d" % i) for i in range(5)]

            def load_chunk(c):
                sl = c % 2
                k.dma(qc[sl][:], qkT_d[c], (), [Bqc[sl]], cka[sl])
                k.dma(kc_[sl][:, KPAD:KPAD + S], qkT_d[4 + c], (), [Bkc[sl]], cka[sl])

            vcount = [0]
            regcnt = [0]

            def load_v(c, d):
                sl = vcount[0] % 3
                vcount[0] += 1
                L = S // d
                nt = L // 128 + 1
                V4 = Vt[sl][:, 0:d * nt, :].rearrange("p (r m) e -> p r m e", m=nt)
                src = va_d[:, 2 * c:2 * c + 2, :].rearrange("(m p r) h e -> p r m (h e)", p=128, r=d)
                k.memset("pool", V4[0:64, :, 0, :], 0.0, [BVt[sl]])
                k.memset("pool", V4[64:128, :, nt - 1, :], 0.0, [BVt[sl]])
                if d <= nt - 1:
                    for r in range(d):
                        k.dma(V4[64:128, r, 0:nt - 1, :], src[0:64, r, :, :], (), [BVt[sl]], cka[2 + sl], q="pool")
                        k.dma(V4[0:64, r, 1:nt, :], src[64:128, r, :, :], (), [BVt[sl]], cka[2 + sl], q="pool")
                else:
                    for m in range(nt - 1):
                        k.dma(V4[64:128, :, m, :], src[0:64, :, m, :], (), [BVt[sl]], cka[2 + sl], q="pool")
                        k.dma(V4[0:64, :, m + 1, :], src[64:128, :, m, :], (), [BVt[sl]], cka[2 + sl], q="pool")
                return sl

            segs = [(c_, pi_) for c_ in range(4) for pi_ in range(3)]
            vslots = {}
            for si_ in range(2):
                vslots[si_] = load_v(segs[si_][0], PATTERNS[segs[si_][1]])
            fin_jobs = []
            load_chunk(0)
            for si, (c, pi) in enumerate(segs):
                d = PATTERNS[pi]
                sl = c % 2
                Oacc = Oaccs[c % 2]
                BOacc = BOaccs[c % 2]
                if pi == 0:
                    if c + 1 < 4:
                        load_chunk(c + 1)
                    k.memset("pool", Oacc[:], 0.0, [b_ for row in BOacc for b_ in row])
                if si + 2 < len(segs):
                    vslots[si + 2] = load_v(segs[si + 2][0], PATTERNS[segs[si + 2][1]])
                vsl = vslots.pop(si)
                if True:
                    L = S // d
                    nt = L // 128 + 1
                    tiles = [(r, m) for r in range(d) for m in range(nt)]
                    pend = {}
                    regs = {}

                    def emit_qk(i):
                        r, m = tiles[i]
                        b0 = 128 if m == 0 else 0
                        b1 = 128 if m == nt - 1 else 256
                        kstart = KPAD + (128 * m - 64) * d + r
                        qstart = (128 * (m - 1) + b0) * d + r
                        nq = b1 - b0
                        Sb2 = S2[i % 2]
                        for hp in range(2):
                            k.mm(Sb2[0][:, hp, b0:b1],
                                 kc_[sl][hp * 64:(hp + 1) * 64, kstart:kstart + 127 * d + 1:d],
                                 qc[sl][hp * 64:(hp + 1) * 64, qstart:qstart + (nq - 1) * d + 1:d],
                                 True, True, [Bkc[sl], Bqc[sl]], [Sb2[1]])
                        pt = PT2.next()
                        k.act(pt[0][:, :, b0:b1], Sb2[0][:, :, b0:b1], AF.Exp, [Sb2[1]], [pt[1]], scale=SC2)
                        k.tt("dve", pt[0][:, :, b0:b1], pt[0][:, :, b0:b1], mask2[:, :, b0:b1], ALU.mult,
                             [pt[1], Bm2], [pt[1]])
                        pend[i] = (pt, b0, b1)

                    def emit_pv(i):
                        r, m = tiles[i]
                        pt, b0, b1 = pend.pop(i)
                        j = r * nt + m
                        ob = Obk.next()
                        nq = b1 - b0
                        for hp in range(2):
                            k.mm(ob[0][0:65, hp, b0:b1], Vt[vsl][:, j, hp * 65:(hp + 1) * 65], pt[0][:, hp, b0:b1],
                                 True, True, [BVt[vsl], pt[1]], [ob[1]])
                        t0 = (128 * (m - 1) + b0) * d + r
                        dst = Oacc[:, :, t0:t0 + (nq - 1) * d + 1:d]
                        bufs_ = oacc_bufs(t0, d, nq, BOacc)
                        k.tt("dve", dst, ob[0][0:65, :, b0:b1], dst, ALU.add, [ob[1]] + bufs_, bufs_)

                    n = len(tiles)
                    SK = 4
                    for i in range(n + SK):
                        if i < n:
                            emit_qk(i)
                        if i >= SK:
                            emit_pv(i - SK)
                        if fin_jobs and i % 2 == 1:
                            fin_jobs.pop(0)()
                if pi == 2:
                    for hp in range(2):
                        for blk in range(8):
                            def fin(hp=hp, blk=blk, c=c, Oacc=Oacc, BOacc=BOacc):
                                t0 = blk * 512
                                fb = finb
                                fbufs = [BOacc[bl][rr_] for bl in range(blk * 4, blk * 4 + 4) for rr_ in range(16)]
                                k.mm(fb[0][0:64, :], onesf[64:65, 0:64], Oacc[64:65, hp, t0:t0 + 512], True, True,
                                     fbufs + [Bm2], [fb[1]])
                                rd = rdn2.next()
                                big_recip(rd[0][:], fb[0][0:64, :], [fb[1]], [rd[1]])
                                ost = oa_st.next()
                                k.tt("dve", ost[0][:], Oacc[0:64, hp, t0:t0 + 512], rd[0][:], ALU.mult, fbufs + [rd[1]], [ost[1]])
                                k.dma(oT_d[c][hp * 64:(hp + 1) * 64, t0:t0 + 512], ost[0][:], [ost[1]], (), stck[4 + (blk % 2)])
                            fin_jobs.append(fin)
            while fin_jobs:
                fin_jobs.pop(0)()
            k.barrier()

    if 3 in phases:
        with ExitStack() as p3:
            SC3 = 1.0 / math.sqrt(192.0)
            for ch in range(2 * NFF):
                k.dma(wupb_d[ch], wup_d[:, ch * 128:(ch + 1) * 128].rearrange("(k p) n -> p k n", p=128), (),
                      [Bwupb], ckwup, q="pool")
            qp3 = [sb(p3, "m_qp%d" % i, [128, S], BF16) for i in range(2)]
            kp2 = sb(p3, "m_kp2", [128, S], BF16)
            Bqp3 = k.bufs(2)
            Bkp3 = k.buf()
            ckm = k.dclock("m0")
            for i in range(2):
                k.dma(qp3[i][:], qpT_d[i], (), [Bqp3[i]], ckm)
            k.dma(kp2[0:64, :], kpT_d, (), [Bkp3], ckm)
            k.dma(kp2[64:128, :], kpT_d, (), [Bkp3], ckm)
            qn3 = [sb(p3, "m_qn%d" % i, [128, S], BF16) for i in range(2)]
            kn3 = [sb(p3, "m_kn%d" % i, [128, S], BF16) for i in range(2)]
            vb3 = [sb(p3, "m_vb%d" % i, [128, NT, 128], BF16) for i in range(2)]
            Bhd = [k.bufs(3) for _ in range(2)]
            ckh = [k.dclock("mh%d" % i) for i in range(2)]

            def load_head(h):
                sl = h % 2
                k.dma(qn3[sl][:], qnT_d[h], (), [Bhd[sl][0]], ckh[sl])
                k.dma(kn3[sl][:], knT_d[h], (), [Bhd[sl][1]], ckh[sl])
                k.dma(vb3[sl][:], vb_d[:, h * 128:(h + 1) * 128].rearrange("(t p) n -> p t n", p=128), (),
                      [Bhd[sl][2]], ckh[sl])

            Sb3 = Rot([(ps(p3, "m_S%d" % i, [128, 512], F32), k.buf()) for i in range(4)])
            Ob3 = Rot([(ps(p3, "m_O%d" % i, [128, 512], F32), k.buf()) for i in range(2)])
            Db3 = Rot([(ps(p3, "m_D%d" % i, [128, 512], F32), k.buf()) for i in range(2)])
            PT3 = Rot([(sb(p3, "m_PT%d" % i, [128, 512], BF16), k.buf()) for i in range(5)])
            rdn3 = Rot([(sb(p3, "m_rd%d" % i, [128, 512], F32), k.buf()) for i in range(2)])
            ob3 = [(sb(p3, "m_ob%d" % i, [128, 512], BF16), k.buf()) for i in range(2)]
            dacc = [[(sb(p3, "m_dacc%d_%d" % (i, a), [128, 512], F32), k.buf()) for a in range(4)] for i in range(2)]
            dsum = [(sb(p3, "m_dsum%d" % i, [128, 512], BF16), k.buf()) for i in range(2)]
            steps = [(h, qb, kt) for h in range(4) for qb in range(8) for kt in range(NT)]
            pend3 = {}
            cur = {}
            load_head(0)

            def emit_s(i):
                h, qb, kt = steps[i]
                sl = h % 2
                hp = h % 2
                Sbk = Sb3.next()
                pt = PT3.next()
                k.mm(Sbk[0][:], kn3[sl][:, kt * 128:(kt + 1) * 128], qn3[sl][:, qb * 512:(qb + 1) * 512],
                     True, False, [Bhd[sl][1], Bhd[sl][0]], [Sbk[1]])
                k.mm(Sbk[0][:], kp2[hp * 64:(hp + 1) * 64, kt * 128:(kt + 1) * 128],
                     qp3[h // 2][hp * 64:(hp + 1) * 64, qb * 512:(qb + 1) * 512],
                     False, True, [Bkp3, Bqp3[h // 2]], [Sbk[1]])
                k.act(pt[0][:], Sbk[0][:], AF.Exp, [Sbk[1]], [pt[1]], scale=SC3)
                pend3[i] = pt

            def emit_pv3(i):
                h, qb, kt = steps[i]
                sl = h % 2
                if kt == 0:
                    cur["O"] = Ob3.next()
                    cur["D"] = Db3.next()
                O = cur["O"]
                Dn = cur["D"]
                pt = pend3.pop(i)
                k.mm(O[0][:], vb3[sl][:, kt, :], pt[0][:], kt == 0, kt == NT - 1, [Bhd[sl][2], pt[1]], [O[1]])
                k.mm(Dn[0][:], ones[:], pt[0][:], kt == 0, kt == NT - 1, [Bc, pt[1]], [Dn[1]])
                if kt == NT - 1:
                    rd = rdn3.next()
                    ob = ob3[qb % 2]
                    big_recip(rd[0][:], Dn[0][:], [Dn[1]], [rd[1]])
                    k.tt("dve", ob[0][:], O[0][:], rd[0][:], ALU.mult, [O[1], rd[1]], [ob[1]])
                    k.dma(oT_d[4 + h][:, qb * 512:(qb + 1) * 512], ob[0][:], [ob[1]], (), stck[6 + (qb % 2)])

            n3 = len(steps)
            load_head(1)
            SK3 = 2
            for i in range(n3 + SK3):
                if i < n3:
                    emit_s(i)
                    cur["hps"] = steps[i][0] % 2
                if i >= SK3:
                    emit_pv3(i - SK3)
                if i < n3 and i >= SK3:
                    h_, qb_, kt_ = steps[i - SK3 + 1]
                    if qb_ == 0 and kt_ == 0 and h_ + 1 < 4 and h_ >= 1:
                        load_head(h_ + 1)
            k.barrier()

    if 4 in phases or 5 in phases:
        p45 = ExitStack()
        x1T = sb(p45, "x1T", [128, 8, S + 2], BF16)
        Bx1T = k.bufs(NT, "x1T")
        Bpad = k.buf("x1Tpad")
        k.memset("pool", x1T[:, :, 0:1], 0.0, [Bpad])
        k.memset("pool", x1T[:, :, S + 1:S + 2], 0.0, [Bpad])
        wdn = sb(p45, "wdn", [128, NFF, D], BF16)
        Bwdn = k.buf("wdn")
        ckw = k.dclock("w45")
        ckwp = k.dclock("w45p")
        for j in range(NFF):
            k.dma(wdn[:, j, :], wdn_d[j * 128:(j + 1) * 128, :], (), [Bwdn], ckwp, q="pool")
        cwt = sb(p45, "cwt", [128, 3, 2 * NFF], F32)
        cbt = sb(p45, "cbt", [128, 2 * NFF], F32)
        Bcw = k.buf("convw")
        for t in range(3):
            k.op("sp", lambda t=t: nc.sync.dma_start(out=cwt[:, t, :], in_=cw_d[t].rearrange("(c p) -> p c", p=128),
                                                     allow_slow_non_contiguous=True), (), [Bcw], clock=ckw)
        k.op("sp", lambda: nc.sync.dma_start(out=cbt[:], in_=cb_d.rearrange("(c p) -> p c", p=128),
                                             allow_slow_non_contiguous=True), (), [Bcw], clock=ckw)
        NXB4 = 4
        xin4 = [sb(p45, "xin4_%d" % i, [128, D], F32) for i in range(NXB4)]
        Bxin4 = k.bufs(NXB4)
        lnsc4 = [ln_scratch(p45, "ln4_%d" % i) for i in range(4)]

        if 4 in phases:
            with ExitStack() as p4:
                ln1g = bcast_vec(p4, "ln1g", ln1g_d, D)
                ln1b = bcast_vec(p4, "ln1b", ln1b_d, D)
                wo = sb(p4, "wo", [128, 8, D], BF16)
                Bw4 = k.buf("w4")
                for c in range(8):
                    k.dma(wo[:, c, :], wo_d[c * 128:(c + 1) * 128, :], (), [Bw4], ckwp, q="pool")
                ong = sb(p4, "ong", [128, 8], F32)
                k.op("sp", lambda: nc.sync.dma_start(out=ong[:], in_=ong_d.rearrange("(c p) -> p c", p=128),
                                                     allow_slow_non_contiguous=True), (), [Bw4], clock=ckw)
                oTs = [sb(p4, "oTs%d" % i, [128, 8, 512], BF16) for i in range(2)]
                BoT = k.bufs(2)
                onb = [sb(p4, "onb%d" % i, [128, 8, 512], BF16) for i in range(2)]
                Bon = [k.bufs(8) for _ in range(2)]
                sqs = Rot([(sb(p4, "sq4_%d" % i, [128, 512], BF16), k.buf()) for i in range(3)])
                rr = [[sb(p4, "rr%d_%d" % (g, i), [128, 512], F32) for i in range(2)] for g in range(2)]
                Brr = [k.bufs(2) for _ in range(2)]
                x1b = [sb(p4, "x1b%d" % i, [128, D], BF16) for i in range(2)]
                Bx1b = k.bufs(2)
                ssq_ps = Rot([(ps(p4, "ssq%d" % i, [128, 512], F32), k.buf()) for i in range(2)])
                y_ps = Rot([(ps(p4, "y4_%d" % i, [128, D], F32), k.buf()) for i in range(2)])
                tp4 = [ps(p4, "tp4_%d" % i, [128, 8, 128], BF16) for i in range(2)]
                Btp4 = k.bufs(2)
                ck4 = [k.dclock("p4_%d" % i) for i in range(2)]

                def load_o(blk):
                    t0 = blk * 512
                    k.dma(oTs[blk % 2][:], oT_d[:, :, t0:t0 + 512].rearrange("c p t -> p c t"), (),
                          [BoT[blk % 2]], ck4[blk % 2])

                def blkgen(blk):
                    sl = blk % 2
                    if blk + 1 < 8:
                        load_o(blk + 1)
                    for g in range(2):
                        bank = ssq_ps.next()
                        for ci in range(4):
                            c = g * 4 + ci
                            sq = sqs.next()
                            k.act(sq[0][:], oTs[sl][:, c, :], AF.Square, [BoT[sl]], [sq[1]])
                            k.mm(bank[0][:], ones[:], sq[0][:], ci == 0, ci == 3, [Bc, sq[1]], [bank[1]])
                            yield
                        r = rr[g][sl]
                        big_rsqrt(r[:], bank[0][:], [bank[1]], [Brr[g][sl]], 1.0 / 512)
                        yield
                        for ci in range(4):
                            c = g * 4 + ci
                            k.stt("dve", onb[sl][:, c, :], oTs[sl][:, c, :], ong[:, c:c + 1], r[:], ALU.mult, ALU.mult,
                                  [BoT[sl], Brr[g][sl], Bw4], [Bon[sl][c]])
                            yield

                def tile4(blk, tt):
                    sl = blk % 2
                    ti = blk * 4 + tt
                    xi = ti % NXB4
                    li = ti % 4
                    k.dma(xin4[xi][:], xn_d[ti * 128:(ti + 1) * 128, :], (), [Bxin4[xi]], ldck[xi])
                    yield
                    y = y_ps.next()
                    for half in range(2):
                        for c in range(8):
                            k.mm(y[0][:, half * 512:(half + 1) * 512], onb[sl][:, c, tt * 128:(tt + 1) * 128],
                                 wo[:, c, half * 512:(half + 1) * 512], c == 0, c == 7, [Bon[sl][c], Bw4], [y[1]])
                    k.stt("dve", xin4[xi][:], xin4[xi][:], ALPHA, y[0][:], ALU.mult, ALU.add,
                          [Bxin4[xi], y[1]], [Bxin4[xi]])
                    yield
                    yield from layer_norm_g(lnsc4[li], xin4[xi][:], xin4[xi][:], ln1g, ln1b, Bxin4[xi], Bxin4[xi])
                    k.dma(x1_d[ti * 128:(ti + 1) * 128, :], xin4[xi][:], [Bxin4[xi]], (), stck[xi])
                    tp = li % 2
                    k.cp("act", x1b[tp][:], xin4[xi][:], [Bxin4[xi]], [Bx1b[tp]])
                    for c in range(8):
                        k.tr(tp4[tp][:, c, :], x1b[tp][:, c * 128:(c + 1) * 128], ident[:], [Bx1b[tp], Bc], [Btp4[tp]])
                    k.cp("act", x1T[:, :, 1 + ti * 128:1 + (ti + 1) * 128], tp4[tp][:], [Btp4[tp]], [Bx1T[ti]])
                    yield

                load_o(0)
                run_mix(blkgen(0), [], 1)
                tiles4 = [(blk, tt) for blk in range(8) for tt in range(4)]
                WIDTH4, STAG4 = 4, 1
                active4 = []
                nxt4 = 0
                bg = None
                bg_blk = 0
                bg_done = 0
                since = STAG4
                while nxt4 < len(tiles4) or active4 or bg is not None:
                    if (nxt4 < len(tiles4) and len(active4) < WIDTH4 and since >= STAG4
                            and tiles4[nxt4][0] <= bg_done):
                        blk, tt = tiles4[nxt4]
                        active4.append(tile4(blk, tt))
                        nxt4 += 1
                        since = 0
                        if tt == 0 and blk + 1 < 8:
                            assert bg is None
                            bg = blkgen(blk + 1)
                            bg_blk = blk + 1
                    since += 1
                    if bg is not None:
                        for _ in range(2):
                            try:
                                next(bg)
                            except StopIteration:
                                bg = None
                                bg_done = bg_blk
                                break
                    for g_ in list(active4):
                        try:
                            next(g_)
                        except StopIteration:
                            active4.remove(g_)
                k.barrier()

        if 5 in phases:
            with ExitStack() as p5:
                ln2g = bcast_vec(p5, "ln2g", ln2g_d, D)
                ln2b = bcast_vec(p5, "ln2b", ln2b_d, D)
                aT = sb(p5, "aT", [128, NFF, 1024], BF16)
                BaT = k.bufs(NFF)
                NWB = 2
                wgv = [sb(p5, "wgv%d" % i, [128, 2, 8, 128], BF16) for i in range(NWB)]
                Bwgv = k.bufs(NWB)
                ckg = [k.dclock("wg%d" % i) for i in range(NWB)]
                u_ps = [(ps(p5, "u%d" % i, [128, 1536], F32), k.buf()) for i in range(2)]
                y5 = (ps(p5, "y5", [128, D], F32), k.buf())
                accg = Rot([(sb(p5, "accg%d" % i, [128, 1024], F32), k.buf()) for i in range(2)])
                accv = Rot([(sb(p5, "accv%d" % i, [128, 1024], F32), k.buf()) for i in range(2)])
                wcount = [0]

                def load_w(j):
                    sl = wcount[0] % NWB
                    wcount[0] += 1
                    for g in range(2):
                        k.dma(wgv[sl][:, g, :, :], wupb_d[g * NFF + j], [Bwupb], [Bwgv[sl]], ckg[sl])
                    return sl

                order = [(qt, j) for qt in range(4) for j in range(NFF)]
                wslots = {}
                wslots[0] = load_w(order[0][1])
                for oi, (qt, j) in enumerate(order):
                    if oi + 1 < len(order):
                        wslots[oi + 1] = load_w(order[oi + 1][1])
                    wsl = wslots.pop(oi)
                    t0 = qt * 1024
                    accs = []
                    for g in range(2):
                        up = u_ps[g]
                        for (c0, c1) in ((0, 512), (512, 1024), (1024, 1026)):
                            for kc in range(8):
                                k.mm(up[0][:, c0:c1], wgv[wsl][:, g, kc, :],
                                     x1T[:, kc, t0 + c0:t0 + c1], kc == 0, kc == 7,
                                     [Bwgv[wsl], Bpad] + Bx1T[max(0, qt * 8 - 1):min(NT, qt * 8 + 9)], [up[1]])
                        ac = (accg if g == 0 else accv).next()
                        ch = g * NFF + j
                        k.act(ac[0][:], up[0][:, 1:1025], AF.Identity, [up[1], Bcw], [ac[1]],
                              bias=cbt[:, ch:ch + 1], scale=cwt[:, 1, ch:ch + 1])
                        k.stt("dve", ac[0][:], up[0][:, 0:1024], cwt[:, 0, ch:ch + 1], ac[0][:], ALU.mult, ALU.add,
                              [up[1], Bcw, ac[1]], [ac[1]])
                        k.stt("dve", ac[0][:], up[0][:, 2:1026], cwt[:, 2, ch:ch + 1], ac[0][:], ALU.mult, ALU.add,
                              [up[1], Bcw, ac[1]], [ac[1]])
                        accs.append(ac)
                    k.act(accs[0][0][:], accs[0][0][:], AF.Silu, [accs[0][1]], [accs[0][1]])
                    k.tt("pool", aT[:, j, :], accs[0][0][:], accs[1][0][:], ALU.mult, [accs[0][1], accs[1][1]], [BaT[j]])
                    if j == NFF - 1:
                        def tile5(tt, qt=qt):
                            ti = qt * 8 + tt
                            xi = ti % NXB4
                            li = ti % 4
                            k.dma(xin4[xi][:], x1_d[ti * 128:(ti + 1) * 128, :], (), [Bxin4[xi]], ldck[xi])
                            for half in range(2):
                                for jj in range(NFF):
                                    k.mm(y5[0][:, half * 512:(half + 1) * 512], aT[:, jj, tt * 128:(tt + 1) * 128],
                                         wdn[:, jj, half * 512:(half + 1) * 512], jj == 0, jj == NFF - 1,
                                         [BaT[jj], Bwdn], [y5[1]])
                            k.stt("dve", xin4[xi][:], xin4[xi][:], ALPHA, y5[0][:], ALU.mult, ALU.add,
                                  [Bxin4[xi], y5[1]], [Bxin4[xi]])
                            yield
                            yield from layer_norm_g(lnsc4[li], xin4[xi][:], xin4[xi][:], ln2g, ln2b, Bxin4[xi], Bxin4[xi])
                            k.dma(out_d[ti * 128:(ti + 1) * 128, :], xin4[xi][:], [Bxin4[xi]], (), stck[xi])
                            yield
                        run_mix(None, [tile5(tt) for tt in range(8)], 3)
                k.barrier()
        p45.close()
    k.barrier()
    return nc, es


def make_consts():
    ident = np.eye(128, dtype=np.float32).astype(ml_dtypes.bfloat16)
    ones = np.ones((128, 128), dtype=np.float32).astype(ml_dtypes.bfloat16)
    a = np.arange(128)[:, None]
    b = np.arange(256)[None, :]
    lq = b - 128
    lk = a - 64
    mask = (np.abs(lq - lk) <= 64).astype(np.float32).astype(ml_dtypes.bfloat16)
    rc = np.zeros((128, 8), dtype=np.float32)
    p = np.arange(128)
    d = p % 64
    invA = np.power(np.float32(THETA), -np.arange(8, dtype=np.float32) * np.float32(2.0 / 16)).astype(np.float32)
    rc[:, 0] = np.where(d < 16, invA[d % 8], 0.0)
    rc[:, 1] = np.where(d < 8, -1.0, 1.0)
    invB = np.power(np.float32(THETA), -np.arange(32, dtype=np.float32) * np.float32(2.0 / 64)).astype(np.float32)
    rc[:, 2] = invB[d % 32]
    rc[:, 3] = np.where(d < 32, -1.0, 1.0)
    return ident, ones, mask, rc


def kernel(**inputs):
    debug = os.environ.get("KDEBUG")
    import time as _t
    _tb = _t.time()
    phs = tuple(int(v) for v in os.environ.get("KPHASES", "1,2,3,4,5").split(","))
    nc, es = build_program(debug=debug, phases=phs)
    if debug:
        print("build time", _t.time() - _tb)
    ident, ones, mask, rc = make_consts()
    x = np.ascontiguousarray(inputs["x"], dtype=np.float32)
    pos = np.ascontiguousarray(inputs["positions"], dtype=np.int32)
    shared = {
        "ln_emb_g": inputs["ln_emb_g"], "ln_emb_b": inputs["ln_emb_b"],
        "w_in": inputs["w_in"][0], "q_norm_g": inputs["q_norm_g"][0], "w_uq": inputs["w_uq"][0],
        "kv_norm_g": inputs["kv_norm_g"][0], "w_ukv": inputs["w_ukv"][0],
        "out_norm_g": inputs["out_norm_g"][0], "w_o": inputs["w_o"][0],
        "ln1_g": inputs["ln1_g"][0], "ln1_b": inputs["ln1_b"][0], "w_up": inputs["w_up"][0],
        "conv_w": inputs["conv_w"][0], "conv_b": inputs["conv_b"][0], "w_down": inputs["w_down"][0],
        "ln2_g": inputs["ln2_g"][0], "ln2_b": inputs["ln2_b"][0],
        "c_ident": ident, "c_ones": ones, "c_mask": mask, "c_rope": rc,
    }
    shared = {kk: np.ascontiguousarray(v) for kk, v in shared.items()}
    in_maps = []
    for b in range(8):
        m = dict(shared)
        m["x"] = x[b]
        m["pos"] = pos[b]
        in_maps.append(m)
    import time as _t
    _t0 = _t.time()
    ncores = int(os.environ.get("KCORES", "8"))
    res = run_bass_kernel_spmd(nc, in_maps[:ncores], core_ids=list(range(ncores)))
    if debug:
        print("run time", _t.time() - _t0)
    es.close()
    if debug:
        return res
    return np.stack([np.asarray(r["out"], dtype=np.float32) for r in res.results], axis=0)
```
